# Optimizing a Trainium2 kernel written in Bass

```python
import math
import jax, jax.numpy as jnp
from jax import lax
import numpy as np

D_MODEL = 1024
BATCH = 4
SEQ = 4096
DEPTH = 4
DEC_BATCH = 16
DEC_SEQ = 32
PAST_LEN = 1024

CHUNK = 64
N_MIXERS = 2
N_CONV = (DEPTH + 1) // 2
N_ATTN = DEPTH // 2
N_HEADS = 8
HEAD_DIM = 64
V_DIM = 2 * HEAD_DIM
D_FF = 2816
CONV_WIDTH = 31
CONV_STATE = CONV_WIDTH - 1
Q_BLOCK = 128
EPS = 1e-6
FFN_RES = 0.5

kernel_name = "streaming_conformer_diffattn_hybrid"


def lambda_init(layer):
    return 0.8 - 0.6 * math.exp(-0.3 * layer)


def rms_norm(x, g):
    x32 = x.astype(jnp.float32)
    y = x32 * lax.rsqrt(jnp.mean(x32 * x32, axis=-1, keepdims=True) + EPS)
    return (y * g.astype(jnp.float32)).astype(x.dtype)


def layer_norm(x, g, b):
    x32 = x.astype(jnp.float32)
    mu = jnp.mean(x32, axis=-1, keepdims=True)
    xc = x32 - mu
    y = xc * lax.rsqrt(jnp.mean(xc * xc, axis=-1, keepdims=True) + EPS)
    return (y * g.astype(jnp.float32) + b.astype(jnp.float32)).astype(x.dtype)


def swiglu(h, w_gate, w_up, w_down):
    return (jax.nn.silu(h @ w_gate) * (h @ w_up)) @ w_down


def conv_module(h, past, w_pw1, b_pw1, w_dw, b_dw, ln_g, ln_b, w_pw2, b_pw2):
    a = h @ w_pw1 + b_pw1
    u = a[..., :D_MODEL] * jax.nn.sigmoid(a[..., D_MODEL:])
    u_ext = jnp.concatenate([past.astype(u.dtype), u], axis=1)
    c = lax.conv_general_dilated(
        u_ext, w_dw[:, None, :].astype(u.dtype), window_strides=(1,), padding='VALID',
        dimension_numbers=('NWC', 'WIO', 'NWC'), feature_group_count=D_MODEL) + b_dw
    c = jax.nn.silu(layer_norm(c, ln_g, ln_b))
    return c @ w_pw2 + b_pw2, u_ext[:, -CONV_STATE:]


def diff_qkv(h, w_qkv):
    B, L, _ = h.shape
    q, k, v = jnp.split(h @ w_qkv, 3, axis=-1)
    q = q.reshape(B, L, 2 * N_HEADS, HEAD_DIM)
    k = k.reshape(B, L, 2 * N_HEADS, HEAD_DIM)
    v = v.reshape(B, L, N_HEADS, V_DIM)
    return q, k, v


def diff_attend(q, k, v, q_pos, k_pos, lam):
    s = jnp.einsum('bqhcd,bkhcd->bhcqk', q, k).astype(jnp.float32) * (HEAD_DIM ** -0.5)
    mask = (k_pos[None, :] // CHUNK) <= (q_pos[:, None] // CHUNK)
    p = jax.nn.softmax(jnp.where(mask, s, -jnp.inf), axis=-1)
    a = p[:, :, 0] - lam * p[:, :, 1]
    return jnp.einsum('bhqk,bkhe->bqhe', a.astype(v.dtype), v)


def diff_attend_blocked(q, k, v, lam):
    B, S = q.shape[:2]
    nb = S // Q_BLOCK
    qb = jnp.moveaxis(q.reshape(B, nb, Q_BLOCK, N_HEADS, 2, HEAD_DIM), 1, 0)
    pos = jnp.arange(S, dtype=jnp.int32)
    posb = pos.reshape(nb, Q_BLOCK)
    k5 = k.reshape(B, S, N_HEADS, 2, HEAD_DIM)
    ob = lax.map(lambda t: diff_attend(t[0], k5, v, t[1], pos, lam), (qb, posb))
    return jnp.moveaxis(ob, 0, 1).reshape(B, S, N_HEADS, V_DIM)


def diff_lambda(lq1, lk1, lq2, lk2, li):
    f32 = jnp.float32
    return (jnp.exp(jnp.sum(lq1.astype(f32) * lk1.astype(f32)))
            - jnp.exp(jnp.sum(lq2.astype(f32) * lk2.astype(f32))) + li)


def diff_out(o, subln, li, w_o):
    B, L = o.shape[:2]
    o = rms_norm(o, subln) * (1.0 - li)
    return o.reshape(B, L, N_HEADS * V_DIM) @ w_o


def trunk(x, cache_k, cache_v, state_conv, p):
    sample = cache_k is not None
    B, L, _ = x.shape
    new_k, new_v, new_c = [], [], []
    for i in range(DEPTH):
        x = x + FFN_RES * swiglu(rms_norm(x, p['ffn1_norm'][i]), p['ffn1_w_gate'][i],
                                 p['ffn1_w_up'][i], p['ffn1_w_down'][i])
        h = rms_norm(x, p['mix_norm'][i])
        j = i // N_MIXERS
        if i % N_MIXERS == 0:
            past = state_conv[j] if sample else jnp.zeros((B, CONV_STATE, D_MODEL), h.dtype)
            m, c_new = conv_module(h, past, p['conv_w_pw1'][j], p['conv_b_pw1'][j], p['conv_w_dw'][j],
                                   p['conv_b_dw'][j], p['conv_ln_g'][j], p['conv_ln_b'][j],
                                   p['conv_w_pw2'][j], p['conv_b_pw2'][j])
            new_c.append(c_new)
        else:
            li = lambda_init(i)
            lam = diff_lambda(p['attn_lambda_q1'][j], p['attn_lambda_k1'][j],
                              p['attn_lambda_q2'][j], p['attn_lambda_k2'][j], li)
            q, k, v = diff_qkv(h, p['attn_w_qkv'][j])
            if sample:
                past_len = cache_k.shape[2]
                k_all = jnp.concatenate([cache_k[j].astype(k.dtype), k], axis=1)
                v_all = jnp.concatenate([cache_v[j].astype(v.dtype), v], axis=1)
                q_pos = past_len + jnp.arange(L, dtype=jnp.int32)
                k_pos = jnp.arange(past_len + L, dtype=jnp.int32)
                o = diff_attend(q.reshape(B, L, N_HEADS, 2, HEAD_DIM),
                                k_all.reshape(B, past_len + L, N_HEADS, 2, HEAD_DIM),
                                v_all, q_pos, k_pos, lam)
            else:
                o = diff_attend_blocked(q, k, v, lam)
            m = diff_out(o, p['attn_subln'][j], li, p['attn_w_o'][j])
            new_k.append(k)
            new_v.append(v)
        x = x + m
        x = x + FFN_RES * swiglu(rms_norm(x, p['ffn2_norm'][i]), p['ffn2_w_gate'][i],
                                 p['ffn2_w_up'][i], p['ffn2_w_down'][i])
    y = rms_norm(x, p['final_norm'])
    return y, jnp.stack(new_k), jnp.stack(new_v), jnp.stack(new_c)


def setup_inputs(seed: int = 0) -> dict:
    key = jax.random.key(seed)
    ks = iter(jax.random.split(key, 32))

    def nrm(shape, scale):
        return jax.random.normal(next(ks), shape, jnp.float32) * scale

    D, F = D_MODEL, D_FF
    return {
        'x_prompt': nrm((BATCH, SEQ, D), 1.0),
        'x_sample': nrm((DEC_BATCH, DEC_SEQ, D), 1.0),
        'cache_k': nrm((N_ATTN, DEC_BATCH, PAST_LEN, 2 * N_HEADS, HEAD_DIM), 1.0),
        'cache_v': nrm((N_ATTN, DEC_BATCH, PAST_LEN, N_HEADS, V_DIM), 1.0),
        'state_conv': nrm((N_CONV, DEC_BATCH, CONV_STATE, D), 0.5),
        'ffn1_norm': 1.0 + nrm((DEPTH, D), 0.01),
        'ffn1_w_gate': nrm((DEPTH, D, F), D ** -0.5),
        'ffn1_w_up': nrm((DEPTH, D, F), D ** -0.5),
        'ffn1_w_down': nrm((DEPTH, F, D), F ** -0.5),
        'mix_norm': 1.0 + nrm((DEPTH, D), 0.01),
        'ffn2_norm': 1.0 + nrm((DEPTH, D), 0.01),
        'ffn2_w_gate': nrm((DEPTH, D, F), D ** -0.5),
        'ffn2_w_up': nrm((DEPTH, D, F), D ** -0.5),
        'ffn2_w_down': nrm((DEPTH, F, D), F ** -0.5),
        'conv_w_pw1': nrm((N_CONV, D, 2 * D), D ** -0.5),
        'conv_b_pw1': nrm((N_CONV, 2 * D), 0.01),
        'conv_w_dw': nrm((N_CONV, CONV_WIDTH, D), CONV_WIDTH ** -0.5),
        'conv_b_dw': nrm((N_CONV, D), 0.01),
        'conv_ln_g': 1.0 + nrm((N_CONV, D), 0.01),
        'conv_ln_b': nrm((N_CONV, D), 0.01),
        'conv_w_pw2': nrm((N_CONV, D, D), D ** -0.5),
        'conv_b_pw2': nrm((N_CONV, D), 0.01),
        'attn_w_qkv': nrm((N_ATTN, D, 3 * D), D ** -0.5),
        'attn_lambda_q1': nrm((N_ATTN, HEAD_DIM), 0.1),
        'attn_lambda_k1': nrm((N_ATTN, HEAD_DIM), 0.1),
        'attn_lambda_q2': nrm((N_ATTN, HEAD_DIM), 0.1),
        'attn_lambda_k2': nrm((N_ATTN, HEAD_DIM), 0.1),
        'attn_subln': 1.0 + nrm((N_ATTN, V_DIM), 0.01),
        'attn_w_o': nrm((N_ATTN, N_HEADS * V_DIM, D), (N_HEADS * V_DIM) ** -0.5),
        'final_norm': 1.0 + nrm((D,), 0.01),
    }


def reference(x_prompt, x_sample, cache_k, cache_v, state_conv,
              ffn1_norm, ffn1_w_gate, ffn1_w_up, ffn1_w_down, mix_norm,
              ffn2_norm, ffn2_w_gate, ffn2_w_up, ffn2_w_down,
              conv_w_pw1, conv_b_pw1, conv_w_dw, conv_b_dw, conv_ln_g, conv_ln_b,
              conv_w_pw2, conv_b_pw2,
              attn_w_qkv, attn_lambda_q1, attn_lambda_k1, attn_lambda_q2, attn_lambda_k2,
              attn_subln, attn_w_o, final_norm):
    p = {
        'ffn1_norm': ffn1_norm, 'ffn1_w_gate': ffn1_w_gate, 'ffn1_w_up': ffn1_w_up,
        'ffn1_w_down': ffn1_w_down, 'mix_norm': mix_norm,
        'ffn2_norm': ffn2_norm, 'ffn2_w_gate': ffn2_w_gate, 'ffn2_w_up': ffn2_w_up,
        'ffn2_w_down': ffn2_w_down,
        'conv_w_pw1': conv_w_pw1, 'conv_b_pw1': conv_b_pw1, 'conv_w_dw': conv_w_dw,
        'conv_b_dw': conv_b_dw, 'conv_ln_g': conv_ln_g, 'conv_ln_b': conv_ln_b,
        'conv_w_pw2': conv_w_pw2, 'conv_b_pw2': conv_b_pw2,
        'attn_w_qkv': attn_w_qkv, 'attn_lambda_q1': attn_lambda_q1, 'attn_lambda_k1': attn_lambda_k1,
        'attn_lambda_q2': attn_lambda_q2, 'attn_lambda_k2': attn_lambda_k2,
        'attn_subln': attn_subln, 'attn_w_o': attn_w_o, 'final_norm': final_norm,
    }
    y_prompt, new_k_prompt, new_v_prompt, new_conv_prompt = trunk(x_prompt, None, None, None, p)
    y_sample, new_k_sample, new_v_sample, new_conv_sample = trunk(x_sample, cache_k, cache_v, state_conv, p)
    return (y_prompt, y_sample, new_k_prompt, new_v_prompt, new_conv_prompt,
            new_k_sample, new_v_sample, new_conv_sample)
```

```python
import math
from contextlib import ExitStack
import numpy as np
import concourse.bass as bass
import concourse.mybir as mybir
from concourse.bass_utils import run_bass_kernel_spmd

F32 = mybir.dt.float32
BF16 = mybir.dt.bfloat16
AF = mybir.ActivationFunctionType
ALU = mybir.AluOpType

D = 1024
KC = 8
DFF = 2816
SEQ = 4096
NSAMP = 2
DSEQ = 32
PAST = 1024
NT = SEQ + NSAMP * DSEQ
NSUP = 4
TS = 1024 + NSAMP * DSEQ
DEPTH = 4
EPS = 1e-6
CW = 31
CS = 30
NSLOT = 6
SLOT = 2048


def NA():
    return DEPTH // 2


def NCV():
    return (DEPTH + 1) // 2


def lambda_init(layer):
    return 0.8 - 0.6 * math.exp(-0.3 * layer)


def const_layout():
    off = {}
    n = 0

    def add(name, w):
        nonlocal n
        off[name] = n
        n += w
    for i in range(DEPTH):
        add(f"ffn1_norm{i}", 8)
        add(f"mix_norm{i}", 8)
        add(f"ffn2_norm{i}", 8)
    add("final_norm", 8)
    for j in range((DEPTH + 1) // 2):
        add(f"b_pw1{j}", 16)
        add(f"b_dw{j}", 8)
        add(f"ln_g{j}", 8)
        add(f"ln_b{j}", 8)
        add(f"b_pw2{j}", 8)
    nres = n
    for j in range((DEPTH + 1) // 2):
        add(f"w_dw{j}", 8 * CW)
    for j in range(DEPTH // 2):
        add(f"subln{j}", 128)
    add("ident", 128)
    for j in range(DEPTH // 2):
        add(f"lq1{j}", 64)
        add(f"lk1{j}", 64)
        add(f"lq2{j}", 64)
        add(f"lk2{j}", 64)
    return off, n, nres


_NC_CACHE = {}
DBG_PHASES = None
DBG_ATTN = 99
FILL_N = 0
COFF, NCONST, NRES = const_layout()


def _configure(seq, depth):
    global SEQ, NT, NSUP, DEPTH, COFF, NCONST, NRES
    SEQ, DEPTH = seq, depth
    NT = SEQ + NSAMP * DSEQ
    NSUP = SEQ // 1024
    COFF, NCONST, NRES = const_layout()
    _NC_CACHE.clear()


class Res:
    __slots__ = ("w", "r", "name", "excl")

    def __init__(self, name="", excl=False):
        self.w = {}
        self.r = {}
        self.name = name
        self.excl = excl


class Ctx:
    def __init__(self, nc, sems, dma_sems, dry):
        self.nc = nc
        self.dry = dry
        self.eng = {"pe": nc.tensor, "act": nc.scalar, "dve": nc.vector,
                    "pool": nc.gpsimd, "sp": nc.sync}
        self.sem = sems
        self.cnt = {k: 0 for k in ("pe", "act", "dve", "pool", "sp")}
        self.waited = {k: {} for k in self.eng}
        self.dma_sems = dma_sems
        self.dma_n = {"sp": 0, "pool": 0}
        self.semobj = {}
        self.ninst = 0

    def _wait(self, E, deps):
        eng = self.eng[E]
        wd = self.waited[E]
        for key, v in deps.items():
            if wd.get(key, 0) < v:
                wd[key] = v
                if not self.dry:
                    eng.wait_ge(self.semobj[key], v)

    def _deps(self, reads, writes):
        deps = {}
        for r in reads:
            for k, v in r.w.items():
                if deps.get(k, 0) < v:
                    deps[k] = v
        for w in writes:
            for k, v in w.w.items():
                if deps.get(k, 0) < v:
                    deps[k] = v
            for k, v in w.r.items():
                if deps.get(k, 0) < v:
                    deps[k] = v
        return deps

    def _mark(self, key, v, reads, writes):
        for r in reads:
            if r.r.get(key, 0) < v:
                r.r[key] = v
        for w in writes:
            if w.w.get(key, 0) < v:
                w.w[key] = v

    def op(self, E, fn, reads=(), writes=(), inc=True):
        ex = [r for r in reads if r.excl]
        if ex:
            writes = list(writes) + ex
        deps = self._deps(reads, writes)
        if E == "pe":
            deps.pop("pe", None)
        self._wait(E, deps)
        self.ninst += 1
        key = E
        inc = True
        if inc:
            self.cnt[E] += 1
            v = self.cnt[E]
            if not self.dry:
                fn(self.eng[E]).then_inc(self.sem[E], 1)
        else:
            v = self.cnt[E] + 1
            if not self.dry:
                fn(self.eng[E])
        self._mark(key, v, reads, writes)

    def dma(self, Q, out, in_, reads=(), writes=()):
        deps = self._deps(reads, writes)
        i = self.dma_n[Q]
        self.dma_n[Q] += 1
        nsem = len(self.dma_sems[Q])
        key = f"d{Q}{i % nsem}"
        val = 16 * (i // nsem + 1)
        if i >= nsem:
            deps[key] = max(deps.get(key, 0), val - 16)
        self._wait(Q, deps)
        self.ninst += 1
        if not self.dry:
            self.eng[Q].dma_start(out=out, in_=in_).then_inc(self.semobj[key], 16)
        self._mark(key, val, reads, writes)
        return key, val


class WQ:
    def __init__(self, ctx, wbuf, plan):
        self.ctx = ctx
        self.wbuf = wbuf
        self.plan = plan
        self.rec = []
        self.i = 0
        self.issued = 0
        self.res = [Res(f"slot{i}") for i in range(NSLOT)]
        self.PF = NSLOT - 4

    def view(self, idx, kch, ncols):
        s = idx % NSLOT
        ap = self.wbuf[:, s * SLOT: s * SLOT + kch * ncols]
        return ap.rearrange("p (k n) -> p k n", k=kch), self.res[s]

    def _issue(self, idx):
        spec = self.plan[idx]
        if spec is None:
            return
        src, kch, ncols = spec
        v, r = self.view(idx, kch, ncols)
        self.ctx.dma("pool", v, src.rearrange("(k p) n -> p k n", p=128), writes=[r])

    def request(self, src, kch, ncols):
        idx = self.i
        self.i += 1
        if self.plan is None:
            self.rec.append(None if src is None else (src, kch, ncols))
        else:
            hi = min(len(self.plan), idx + self.PF + 1)
            while self.issued < hi:
                self._issue(self.issued)
                self.issued += 1
        return self.view(idx, kch, ncols)


class Rot:
    def __init__(self, items):
        self.items = items
        self.i = 0

    def next(self):
        it = self.items[self.i % len(self.items)]
        self.i += 1
        return it


def build_program():
    nc = bass.Bass("TRN2", target_bir_lowering=False, dynamic_dma_scratch_size=None)
    dt = nc.dram_tensor
    xT_d = dt("xT", [128, KC, NT], F32, kind="ExternalInput").ap()
    ck_d = dt("ck", [NA(), NSAMP, 8, 128, PAST], F32, kind="ExternalInput").ap()
    cv_d = dt("cv", [NA(), NSAMP, PAST, D], F32, kind="ExternalInput").ap()
    sc_d = dt("sc", [NCV(), NSAMP, 128, KC, CS], F32, kind="ExternalInput").ap()
    cst_d = dt("cst", [128, NCONST], F32, kind="ExternalInput").ap()
    W = {}
    for nm, shp in (("ffn1_w_gate", [DEPTH, D, DFF]), ("ffn1_w_up", [DEPTH, D, DFF]),
                    ("ffn1_w_down", [DEPTH, DFF, D]), ("ffn2_w_gate", [DEPTH, D, DFF]),
                    ("ffn2_w_up", [DEPTH, D, DFF]), ("ffn2_w_down", [DEPTH, DFF, D]),
                    ("conv_w_pw1", [NCV(), D, 2 * D]), ("conv_w_pw2", [NCV(), D, D]),
                    ("attn_w_qkv", [NA(), D, 3 * D]), ("attn_w_o", [NA(), D, D])):
        W[nm] = dt(nm, shp, F32, kind="ExternalInput").ap()
    yT_d = dt("yT", [128, KC, NT], F32, kind="ExternalOutput").ap()
    kT_d = dt("kT_out", [NA(), 8, 128, NT], F32, kind="ExternalOutput").ap()
    v_d = dt("v_out", [NA(), NT, D], F32, kind="ExternalOutput").ap()
    cv_o = dt("conv_out", [NCV(), 3, 128, KC, CS], F32, kind="ExternalOutput").ap()
    ks_d = dt("k_scr", [8, 128, NT], BF16, kind="Internal").ap()
    vs_d = dt("v_scr", [NT, D], BF16, kind="Internal").ap()

    with ExitStack() as es:
        def sb(name, shape, dtype):
            return es.enter_context(nc.sbuf_tensor(name, shape, dtype))

        def ps(name, shape, dtype):
            return es.enter_context(nc.psum_tensor(name, shape, dtype))

        x = sb("x_sb", [128, KC, NT], F32)
        H = sb("H", [128, KC, TS], BF16)
        A = sb("A", [128, KC * TS], BF16)
        wbuf = sb("wbuf", [128, NSLOT * SLOT], BF16)
        cst = sb("cst_sb", [128, NRES], F32)
        ident = sb("ident", [128, 128], BF16)
        ones = sb("ones", [128, 128], BF16)
        tf = [sb(f"tf{i}", [128, 512], F32) for i in range(3)]
        tb = [sb(f"tb{i}", [128, 512], BF16) for i in range(2)]
        rstd_t = sb("rstd", [128, 512], F32)
        mean_t = sb("mean", [128, 512], F32)
        o0 = mean_t[:, :].rearrange("p (q e) -> p q e", q=4)
        ph = sb("ph", [128, 1464], F32)
        uo = ph[:, 0:720].rearrange("p (k s n) -> p k s n", k=KC, s=3)
        wdw_v = ph[:, 720:968]
        Us = ph[:, 968:1464].bitcast(BF16).rearrange("p (k s n) -> p k s n", k=KC, s=NSAMP)
        ktile = [ph[:, 256 * i:256 * (i + 1)].bitcast(BF16) for i in range(2)]
        vtile = [ph[:, 512 + 260 * i:512 + 260 * (i + 1)].bitcast(BF16).rearrange("p (t e) -> p t e", t=4) for i in range(2)]
        subln_v = ph[:, 1032:1160]
        onb = ph[:, 1160:1416].bitcast(BF16)
        kown = sb("kown", [128, 32], BF16)
        vown = sb("vown", [32, 130], BF16)
        small = sb("small", [128, 64], F32)
        banks = [ps(f"bk{i}", [128, 512], F32) for i in range(7)]
        bankT = ps("bkT", [128, 1024], BF16)

        sems = {k: es.enter_context(nc.semaphore(f"s_{k}")) for k in ("pe", "act", "dve", "pool", "sp")}
        dma_sems = {q: [es.enter_context(nc.semaphore(f"d{q}{i}")) for i in range(8)] for q in ("sp", "pool")}

        plan = None
        for dry in (True, False):
            ctx = Ctx(nc, sems, dma_sems, dry)
            for k, s in sems.items():
                ctx.semobj[k] = s
            for q in ("sp", "pool"):
                for i, s in enumerate(dma_sems[q]):
                    ctx.semobj[f"d{q}{i}"] = s
            wq = WQ(ctx, wbuf, plan)
            emit(ctx, wq, locals())
            if dry:
                plan = wq.rec
    return nc


def emit(ctx, wq, L):
    x, H, A, cst = L["x"], L["H"], L["A"], L["cst"]
    ident, ones = L["ident"], L["ones"]
    banks, bankT = L["banks"], L["bankT"]
    W = L["W"]
    xT_d, ck_d, cv_d, sc_d, cst_d = L["xT_d"], L["ck_d"], L["cv_d"], L["sc_d"], L["cst_d"]
    yT_d, kT_d, v_d, cv_o, ks_d, vs_d = L["yT_d"], L["kT_d"], L["v_d"], L["cv_o"], L["ks_d"], L["vs_d"]
    o0, small, uo, kown, vown = L["o0"], L["small"], L["uo"], L["kown"], L["vown"]
    rstd_t, mean_t = L["rstd_t"], L["mean_t"]
    wdw_v, Us, subln_v = L["wdw_v"], L["Us"], L["subln_v"]
    op, dma = ctx.op, ctx.dma

    R = {}

    def res(name):
        if name not in R:
            R[name] = Res(name)
        return R[name]
    bank_r = [res(f"bank{i}") for i in range(7)]
    bankT_r = res("bankT")
    for b_ in bank_r + [bankT_r]:
        b_.excl = True
    tf = Rot(list(zip(L["tf"], [res(f"tf{i}") for i in range(3)])))
    tb = Rot(list(zip(L["tb"], [res(f"tb{i}") for i in range(2)])))
    ktl = Rot(list(zip(L["ktile"], [res(f"kt{i}") for i in range(2)])))
    vtl = Rot(list(zip(L["vtile"], [res(f"vt{i}") for i in range(2)])))
    gbank = Rot([(banks[0], bank_r[0]), (banks[1], bank_r[1])])
    ubank = Rot([(banks[2], bank_r[2]), (banks[3], bank_r[3])])
    dbank = Rot([(banks[4], bank_r[4]), (banks[5], bank_r[5])])
    sbank = (banks[6], bank_r[6])
    cres = res("cst")
    tf2rot = Rot(tf.items[0:2])
    evac_flip = [0]
    evac_ctr = [0]
    eps_c = small[:, 16:17]
    onb = Rot([(L["onb"][:, 128 * i:128 * (i + 1)], res(f"onb{i}")) for i in range(4)])
    idres = res("ident")

    def cc(name, k=0, w=1):
        o = COFF[name] + k
        return cst[:, o:o + w]

    def merge(dst, srcs):
        for sr in srcs:
            for k, v in sr.w.items():
                if dst.w.get(k, 0) < v:
                    dst.w[k] = v
            for k, v in sr.r.items():
                if dst.r.get(k, 0) < v:
                    dst.r[k] = v

    def a_alias(kind):
        allr = [res("a0"), res("a512"), res("a1024"), res("Up"), res("Aq")]
        tgt = {"ffn": allr[0:3], "conv": [allr[3]], "attn": [allr[4]]}[kind]
        for t in tgt:
            merge(t, allr)

    def ph_alias(kind):
        allr = [res("uo"), res("wdw"), res("Us"), res("kt0"), res("kt1"), res("vt0"), res("vt1"), res("subln")] + [res(f"onb{i}") for i in range(4)]
        tgt = allr[0:3] if kind == "conv" else allr[3:]
        for t in tgt:
            merge(t, allr)

    def subs_of(s):
        out = [(1024 * s, 0, 512, "p"), (1024 * s + 512, 512, 512, "p")]
        if s == 0:
            out.append((SEQ, 1024, NSAMP * DSEQ, "s"))
        return out

    def xr(xcol):
        return res(f"x{xcol}")

    def hr(lcol):
        return res(f"h{lcol}")

    def ar(lcol):
        return res(f"a{lcol}")

    dma("sp", cst[:, :], cst_d[:, 0:NRES], writes=[cres])
    for k in range(KC):
        dma("sp", x[:, k, :], xT_d[:, k, :], writes=[xr(c) for s in range(NSUP) for (c, _, _, _) in subs_of(s)])
    op("pool", lambda e: e.memset(ones[:, :], 1.0), writes=[idres])
    op("pool", lambda e: e.memset(small[:, 16:17], EPS), writes=[res("small")])
    it, itr = tf.next()
    dma("sp", it[:, 0:128], cst_d[:, COFF["ident"]:COFF["ident"] + 128], writes=[itr])
    op("dve", lambda e: e.tensor_copy(out=ident[:, :], in_=it[:, 0:128]), reads=[itr], writes=[idres])
    vor = res("vown")
    op("pool", lambda e: e.memset(vown[:, 128:130], 1.0), writes=[vor])

    def rmsnorm(s, gname, final=False):
        for (xcol, lcol, n, kind) in subs_of(s):
            sbk, sbr = sbank
            for k in range(KC):
                t, tr = tb.next()
                op("act", lambda e, t=t, k=k: e.activation(out=t[:, :n], in_=x[:, k, xcol:xcol + n], func=AF.Square),
                   reads=[xr(xcol)], writes=[tr])
                op("pe", lambda e, t=t, k=k: e.matmul(sbk[:, :n], lhsT=ones[:, :], rhs=t[:, :n], start=(k == 0), stop=(k == KC - 1)),
                   reads=[tr, idres], writes=[sbr], inc=(k == KC - 1))
            rr = res("rstd")
            op("act", lambda e: e.activation(out=rstd_t[:, :n], in_=sbk[:, :n], func=AF.Sqrt, scale=1.0 / D, bias=EPS),
               reads=[sbr], writes=[rr])
            op("dve", lambda e: e.reciprocal(out=rstd_t[:, :n], in_=rstd_t[:, :n]), reads=[rr], writes=[rr])
            for k in range(KC):
                if not final:
                    op("dve", lambda e, k=k: e.scalar_tensor_tensor(out=H[:, k, lcol:lcol + n], in0=x[:, k, xcol:xcol + n],
                                                                    scalar=cc(gname, k), in1=rstd_t[:, :n], op0=ALU.mult, op1=ALU.mult),
                       reads=[xr(xcol), rr, cres], writes=[hr(lcol)])
                else:
                    t, tr = tf.next()
                    op("dve", lambda e, k=k, t=t: e.scalar_tensor_tensor(out=t[:, :n], in0=x[:, k, xcol:xcol + n],
                                                                         scalar=cc(gname, k), in1=rstd_t[:, :n], op0=ALU.mult, op1=ALU.mult),
                       reads=[xr(xcol), rr, cres], writes=[tr])
                    dma("sp", yT_d[:, k, xcol:xcol + n], t[:, :n], reads=[tr], writes=[res("yout")])

    def proj_fm(s, wsrc, col0, nchunks, evac):
        for g in range(nchunks // 2):
            wv, wr = wq.request(wsrc[:, col0 + 256 * g: col0 + 256 * g + 256], KC, 256)
            for sub in subs_of(s):
                (xcol, lcol, n, kind) = sub
                for jj in range(2):
                    bk, br = dbank.next()
                    for k in range(KC):
                        op("pe", lambda e, k=k, bk=bk, jj=jj: e.matmul(bk[:, :n], lhsT=wv[:, k, jj * 128:(jj + 1) * 128], rhs=H[:, k, lcol:lcol + n],
                                                                       start=(k == 0), stop=(k == KC - 1)),
                           reads=[wr, hr(lcol)], writes=[br], inc=(k == KC - 1))
                    evac(2 * g + jj, sub, bk, br)

    def resid_evac(bias_name=None, scale=1.0):
        def f(c, sub, bk, br):
            (xcol, lcol, n, kind) = sub
            if bias_name is None:
                op("dve", lambda e: e.scalar_tensor_tensor(out=x[:, c, xcol:xcol + n], in0=bk[:, :n], scalar=scale,
                                                           in1=x[:, c, xcol:xcol + n], op0=ALU.mult, op1=ALU.add),
                   reads=[br, xr(xcol)], writes=[xr(xcol)])
            else:
                op("dve", lambda e: e.scalar_tensor_tensor(out=x[:, c, xcol:xcol + n], in0=bk[:, :n], scalar=cc(bias_name, c),
                                                           in1=x[:, c, xcol:xcol + n], op0=ALU.add, op1=ALU.add),
                   reads=[br, xr(xcol), cres], writes=[xr(xcol)])
        return f

    def ffn(s, layer, which):
        a_alias("ffn")
        rmsnorm(s, f"{which}_norm{layer}")
        wg, wu, wd = W[f"{which}_w_gate"][layer], W[f"{which}_w_up"][layer], W[f"{which}_w_down"][layer]
        Av = A[:, 0:2 * TS].rearrange("p (j n) -> p j n", j=2)
        dq = []

        def emit_d(n_units):
            for _ in range(min(n_units, len(dq))):
                (dv, dr, xcol, lcol, n, d) = dq.pop(0)
                bk, br = dbank.next()
                for jj in range(2):
                    op("pe", lambda e, jj=jj: e.matmul(bk[:, :n], lhsT=dv[:, jj, d * 128:(d + 1) * 128], rhs=Av[:, jj, lcol:lcol + n],
                                                       start=(jj == 0), stop=(jj == 1)),
                       reads=[dr, ar(lcol)], writes=[br])
                op("dve", lambda e: e.scalar_tensor_tensor(out=x[:, d, xcol:xcol + n], in0=bk[:, :n], scalar=0.5,
                                                           in1=x[:, d, xcol:xcol + n], op0=ALU.mult, op1=ALU.add),
                   reads=[br, xr(xcol)], writes=[xr(xcol)])

        for g in range(DFF // 256):
            gv, gr = wq.request(wg[:, 256 * g:256 * g + 256], KC, 256)
            uv, ur = wq.request(wu[:, 256 * g:256 * g + 256], KC, 256)
            dv, dr = wq.request(wd[256 * g:256 * g + 256, :], 2, D)
            for (xcol, lcol, n, kind) in subs_of(s):
                pend_here = sum(1 for q_ in dq if q_[3] == lcol)
                if pend_here:
                    emit_d(len(dq))
                for jj in range(2):
                    gb, gbr = gbank.next()
                    ub, ubr = ubank.next()
                    for k in range(KC):
                        op("pe", lambda e, k=k: e.matmul(gb[:, :n], lhsT=gv[:, k, jj * 128:(jj + 1) * 128], rhs=H[:, k, lcol:lcol + n],
                                                         start=(k == 0), stop=(k == KC - 1)),
                           reads=[gr, hr(lcol)], writes=[gbr])
                    for k in range(KC):
                        op("pe", lambda e, k=k: e.matmul(ub[:, :n], lhsT=uv[:, k, jj * 128:(jj + 1) * 128], rhs=H[:, k, lcol:lcol + n],
                                                         start=(k == 0), stop=(k == KC - 1)),
                           reads=[ur, hr(lcol)], writes=[ubr])
                    t, tr = tf.next()
                    op("act", lambda e: e.activation(out=t[:, :n], in_=gb[:, :n], func=AF.Silu), reads=[gbr], writes=[tr])
                    op("dve", lambda e: e.tensor_tensor(out=Av[:, jj, lcol:lcol + n], in0=t[:, :n], in1=ub[:, :n], op=ALU.mult),
                       reads=[tr, ubr], writes=[ar(lcol)])
                    emit_d(4)
                for d in range(KC):
                    dq.append((dv, dr, xcol, lcol, n, d))
            if len(dq) > 2 * KC:
                emit_d(len(dq) - 2 * KC)
        emit_d(len(dq))

    UPW = 30 + 1024
    Up = A[:, 0:KC * UPW].rearrange("p (k n) -> p k n", k=KC)
    upr = res("Up")
    usr = res("Us")
    wdr = res("wdw")

    def conv_layer(s, layer):
        j = layer // 2
        a_alias("conv")
        ph_alias("conv")
        rmsnorm(s, f"mix_norm{layer}")
        if s == 0:
            dma("sp", wdw_v, cst_d[:, COFF[f"w_dw{j}"]:COFF[f"w_dw{j}"] + 8 * CW], writes=[wdr])
            op("pool", lambda e: e.memset(Up[:, :, 0:CS], 0.0), writes=[upr])
            for q in range(NSAMP):
                dma("pool", Us[:, :, q, 0:CS], sc_d[j, q], writes=[usr])
        else:
            op("dve", lambda e: e.tensor_copy(out=Up[:, :, 0:CS], in_=Up[:, :, 1024:1024 + CS]), reads=[upr], writes=[upr])
        w1 = W["conv_w_pw1"][j]
        uor = res("uo")
        for g in range(4):
            vv, vr = wq.request(w1[:, 256 * g:256 * g + 256], KC, 256)
            gv, gr = wq.request(w1[:, D + 256 * g:D + 256 * g + 256], KC, 256)
            for (xcol, lcol, n, kind) in subs_of(s):
                for jj in range(2):
                    c = 2 * g + jj
                    vb, vbr = gbank.next()
                    gb, gbr = ubank.next()
                    for k in range(KC):
                        op("pe", lambda e, k=k, vb=vb, jj=jj: e.matmul(vb[:, :n], lhsT=vv[:, k, jj * 128:(jj + 1) * 128], rhs=H[:, k, lcol:lcol + n],
                                                                       start=(k == 0), stop=(k == KC - 1)),
                           reads=[vr, hr(lcol)], writes=[vbr], inc=(k == KC - 1))
                    for k in range(KC):
                        op("pe", lambda e, k=k, gb=gb, jj=jj: e.matmul(gb[:, :n], lhsT=gv[:, k, jj * 128:(jj + 1) * 128], rhs=H[:, k, lcol:lcol + n],
                                                                       start=(k == 0), stop=(k == KC - 1)),
                           reads=[gr, hr(lcol)], writes=[gbr], inc=(k == KC - 1))
                    t, tr = tf.next()
                    op("act", lambda e, t=t, gb=gb, c=c: e.activation(out=t[:, :n], in_=gb[:, :n], func=AF.Sigmoid, bias=cc(f"b_pw1{j}", 8 + c)),
                       reads=[gbr, cres], writes=[tr])
                    if kind == "p":
                        op("dve", lambda e, t=t, vb=vb, c=c: e.scalar_tensor_tensor(out=Up[:, c, CS + lcol:CS + lcol + n], in0=vb[:, :n], scalar=cc(f"b_pw1{j}", c),
                                                                                    in1=t[:, :n], op0=ALU.add, op1=ALU.mult),
                           reads=[tr, vbr, cres], writes=[upr])
                        if s == NSUP - 1 and lcol == 512:
                            op("dve", lambda e, t=t, vb=vb, c=c: e.scalar_tensor_tensor(out=uo[:, c, 0, :], in0=vb[:, 512 - CS:512], scalar=cc(f"b_pw1{j}", c),
                                                                                        in1=t[:, 512 - CS:512], op0=ALU.add, op1=ALU.mult),
                               reads=[tr, vbr, cres], writes=[uor])
                    else:
                        for q in range(NSAMP):
                            op("dve", lambda e, t=t, vb=vb, c=c, q=q: e.scalar_tensor_tensor(out=Us[:, c, q, CS:CS + DSEQ], in0=vb[:, q * DSEQ:(q + 1) * DSEQ], scalar=cc(f"b_pw1{j}", c),
                                                                                             in1=t[:, q * DSEQ:(q + 1) * DSEQ], op0=ALU.add, op1=ALU.mult),
                               reads=[tr, vbr, cres], writes=[usr])
                            op("dve", lambda e, t=t, vb=vb, c=c, q=q: e.scalar_tensor_tensor(out=uo[:, c, 1 + q, :], in0=vb[:, q * DSEQ + 2:(q + 1) * DSEQ], scalar=cc(f"b_pw1{j}", c),
                                                                                             in1=t[:, q * DSEQ + 2:(q + 1) * DSEQ], op0=ALU.add, op1=ALU.mult),
                               reads=[tr, vbr, cres], writes=[uor])
        if s == NSUP - 1:
            dma("sp", cv_o[j, 0], uo[:, :, 0, :], reads=[uor], writes=[res("cvout")])
        if s == 0:
            for q in range(NSAMP):
                dma("sp", cv_o[j, 1 + q], uo[:, :, 1 + q, :], reads=[uor], writes=[res("cvout")])
        for c in range(KC):
            dgs = []
            for (t0, t1) in ((0, 16), (16, CW)):
                dgv, dgr = wq.request(None, 16, 128)
                for tap in range(t0, t1):
                    op("dve", lambda e, tap=tap, dgv=dgv, t0=t0: e.tensor_scalar(out=dgv[:, tap - t0, :], in0=ident[:, :], scalar1=wdw_v[:, c * CW + tap:c * CW + tap + 1],
                                                                                 scalar2=None, op0=ALU.mult),
                       reads=[idres, wdr], writes=[dgr])
                dgs.append((dgv, dgr, t0, t1))
            for (xcol, lcol, n, kind) in subs_of(s):
                units = [(None, lcol, n)] if kind == "p" else [(q, lcol + q * DSEQ, DSEQ) for q in range(NSAMP)]
                for (q, lc, nn) in units:
                    bk, br = dbank.next()
                    for (dgv, dgr, t0, t1) in dgs:
                        for tap in range(t0, t1):
                            rhs = Up[:, c, lc + tap:lc + tap + nn] if q is None else Us[:, c, q, tap:tap + nn]
                            op("pe", lambda e, tap=tap, dgv=dgv, t0=t0, rhs=rhs, bk=bk: e.matmul(bk[:, :nn], lhsT=dgv[:, tap - t0, :], rhs=rhs,
                                                                                                 start=(tap == 0), stop=(tap == CW - 1)),
                               reads=[dgr, upr if q is None else usr], writes=[br], inc=(tap == CW - 1))
                    op("act", lambda e, bk=bk, lc=lc, nn=nn: e.activation(out=H[:, c, lc:lc + nn], in_=bk[:, :nn], func=AF.Identity, bias=cc(f"b_dw{j}", c)),
                       reads=[br, cres], writes=[hr(lcol)])
        b1, b1r = gbank.next()
        b2, b2r = ubank.next()
        for (xcol, lcol, n, kind) in subs_of(s):
            for k in range(KC):
                t, tr = tb.next()
                op("act", lambda e, t=t, k=k: e.activation(out=t[:, :n], in_=H[:, k, lcol:lcol + n], func=AF.Square), reads=[hr(lcol)], writes=[tr])
                op("pe", lambda e, k=k: e.matmul(b1[:, :n], lhsT=ones[:, :], rhs=H[:, k, lcol:lcol + n], start=(k == 0), stop=(k == KC - 1)),
                   reads=[hr(lcol), idres], writes=[b1r], inc=(k == KC - 1))
                op("pe", lambda e, k=k, t=t: e.matmul(b2[:, :n], lhsT=ones[:, :], rhs=t[:, :n], start=(k == 0), stop=(k == KC - 1)),
                   reads=[tr, idres], writes=[b2r], inc=(k == KC - 1))
            mr, rr = res("mean"), res("rstd")
            t, tr = tf.next()
            op("dve", lambda e: e.tensor_scalar(out=mean_t[:, :n], in0=b1[:, :n], scalar1=1.0 / D, scalar2=None, op0=ALU.mult), reads=[b1r], writes=[mr])
            op("dve", lambda e, t=t: e.tensor_tensor(out=t[:, :n], in0=mean_t[:, :n], in1=mean_t[:, :n], op=ALU.mult), reads=[mr], writes=[tr])
            op("dve", lambda e, t=t: e.scalar_tensor_tensor(out=rstd_t[:, :n], in0=b2[:, :n], scalar=1.0 / D, in1=t[:, :n], op0=ALU.mult, op1=ALU.subtract),
               reads=[b2r, tr], writes=[rr])
            op("act", lambda e: e.activation(out=rstd_t[:, :n], in_=rstd_t[:, :n], func=AF.Sqrt, bias=EPS), reads=[rr], writes=[rr])
            op("dve", lambda e: e.reciprocal(out=rstd_t[:, :n], in_=rstd_t[:, :n]), reads=[rr], writes=[rr])
            for k in range(KC):
                t, tr = tf.next()
                t2, tr2 = tf.next()
                op("dve", lambda e, t=t, k=k: e.tensor_tensor(out=t[:, :n], in0=H[:, k, lcol:lcol + n], in1=mean_t[:, :n], op=ALU.subtract),
                   reads=[hr(lcol), mr], writes=[tr])
                op("dve", lambda e, t=t, t2=t2: e.tensor_tensor(out=t2[:, :n], in0=t[:, :n], in1=rstd_t[:, :n], op=ALU.mult), reads=[tr, rr], writes=[tr2])
                op("act", lambda e, t2=t2, k=k: e.activation(out=H[:, k, lcol:lcol + n], in_=t2[:, :n], func=AF.Silu, scale=cc(f"ln_g{j}", k), bias=cc(f"ln_b{j}", k)),
                   reads=[tr2, cres], writes=[hr(lcol)])
        proj_fm(s, W["conv_w_pw2"][j], 0, KC, resid_evac(bias_name=f"b_pw2{j}"))

    Aq = A[:, 0:KC * TS].rearrange("p (h n) -> p h n", h=KC)
    aqr = res("Aq")
    smr = res("small")

    def attn_lambda(layer):
        j = layer // 2
        lt, ltr = tf.next()
        dma("sp", lt[:, 0:256], cst_d[:, COFF[f"lq1{j}"]:COFF[f"lq1{j}"] + 256], writes=[ltr])
        for i in range(2):
            t, tr = tf.next()
            op("dve", lambda e, t=t: e.tensor_tensor(out=t[:, :64], in0=lt[:, 128 * i:128 * i + 64], in1=lt[:, 128 * i + 64:128 * i + 128], op=ALU.mult), reads=[ltr], writes=[tr])
            op("dve", lambda e, t=t, i=i: e.reduce_sum(out=small[:, i:i + 1], in_=t[:, :64], axis=mybir.AxisListType.X), reads=[tr], writes=[smr])
            op("act", lambda e, i=i: e.activation(out=small[:, 2 + i:3 + i], in_=small[:, i:i + 1], func=AF.Exp), reads=[smr], writes=[smr])
        li = lambda_init(layer)
        op("dve", lambda e: e.scalar_tensor_tensor(out=small[:, 4:5], in0=small[:, 3:4], scalar=-li, in1=small[:, 2:3], op0=ALU.add, op1=ALU.subtract),
           reads=[smr], writes=[smr])

    def evac_acc(accs, nq, cmap, hh, lcols, layer):
        li = lambda_init(layer)
        o0r = res("mean")
        nqt = len(accs)
        W_ = 128 * nqt
        for qi, (bk, br) in enumerate(accs):
            op("dve", lambda e, bk=bk, qi=qi: e.reciprocal(out=small[:nq, 20 + qi:21 + qi], in_=bk[:nq, 128:129]), reads=[br], writes=[smr])
        if cmap == 0:
            for qi, (bk, br) in enumerate(accs):
                op("dve", lambda e, bk=bk, qi=qi: e.tensor_scalar(out=o0[:nq, qi, :], in0=bk[:nq, 0:128], scalar1=small[:nq, 20 + qi:21 + qi], scalar2=None, op0=ALU.mult),
                   reads=[br, smr], writes=[o0r])
            return []
        ta, tar = tf2rot.next()
        tb_, tbr_ = tf2rot.next()
        for qi, (bk, br) in enumerate(accs):
            op("dve", lambda e, bk=bk, qi=qi: e.tensor_scalar(out=ta[:nq, 128 * qi:128 * qi + 128], in0=bk[:nq, 0:128], scalar1=small[:nq, 20 + qi:21 + qi],
                                                              scalar2=small[:nq, 4:5], op0=ALU.mult, op1=ALU.mult),
               reads=[br, smr], writes=[tar])
        o0f = mean_t[:nq, 0:W_]
        op("dve", lambda e: e.tensor_tensor(out=ta[:nq, 0:W_], in0=ta[:nq, 0:W_], in1=o0f, op=ALU.add), reads=[tar, o0r], writes=[tar])
        op("dve", lambda e: e.tensor_tensor(out=tb_[:nq, 0:W_], in0=ta[:nq, 0:W_], in1=ta[:nq, 0:W_], op=ALU.mult), reads=[tar], writes=[tbr_])
        ssum = small[:nq, 48 + 4 * (evac_ctr[0] % 2):48 + 4 * (evac_ctr[0] % 2) + nqt]
        op("dve", lambda e: e.reduce_sum(out=ssum, in_=tb_[:nq, 0:W_].rearrange("p (q e) -> p q e", q=nqt), axis=mybir.AxisListType.X),
           reads=[tbr_], writes=[smr])
        ob = L["onb"]
        obr = res("onb0")
        lcol0 = lcols[0]
        sm_lo = 32 + 8 * (evac_ctr[0] % 2)
        evac_ctr[0] += 1

        def stage3():
            for qi in range(nqt):
                op("pe", lambda e, qi=qi: e.transpose(bankT[:, nq * qi:nq * qi + nq], ob[:nq, 128 * qi:128 * qi + 128], ident[:nq, :nq]), reads=[obr, idres], writes=[bankT_r])
            op("dve", lambda e: e.tensor_copy(out=H[:, hh, lcol0:lcol0 + nq * nqt], in_=bankT[:, 0:nq * nqt]), reads=[bankT_r],
               writes=[hr(lcol0 - lcol0 % 512 if lcol0 < 1024 else 1024)])
            return []

        def stage2():
            op("act", lambda e: e.activation(out=small[:nq, sm_lo:sm_lo + nqt], in_=small[:nq, 24:24 + nqt] if False else ssum, func=AF.Ln, scale=1.0 / 128, bias=eps_c[:nq, :]),
               reads=[smr], writes=[smr])
            op("act", lambda e: e.activation(out=small[:nq, sm_lo + 4:sm_lo + 4 + nqt], in_=small[:nq, sm_lo:sm_lo + nqt], func=AF.Exp, scale=-0.5), reads=[smr], writes=[smr])
            op("dve", lambda e: e.tensor_scalar(out=small[:nq, sm_lo + 4:sm_lo + 4 + nqt], in0=small[:nq, sm_lo + 4:sm_lo + 4 + nqt], scalar1=(1.0 - li), scalar2=None, op0=ALU.mult),
               reads=[smr], writes=[smr])
            for qi in range(nqt):
                op("dve", lambda e, qi=qi: e.scalar_tensor_tensor(out=ob[:nq, 128 * qi:128 * qi + 128], in0=ta[:nq, 128 * qi:128 * qi + 128], scalar=small[:nq, sm_lo + 4 + qi:sm_lo + 5 + qi],
                                                                  in1=subln_v[:nq, :], op0=ALU.mult, op1=ALU.mult),
                   reads=[tar, smr, res("subln")], writes=[obr])
            return [(4, stage3)]
        return [(5, stage2)]

    def attn_layer(s, layer):
        j = layer // 2
        wqkv = W["attn_w_qkv"][j]
        a_alias("attn")
        ph_alias("attn")
        if s == 0:
            attn_lambda(layer)
            dma("sp", subln_v, cst_d[:, COFF[f"subln{j}"]:COFF[f"subln{j}"] + 128], writes=[res("subln")])
            for vi in range(2):
                op("pool", lambda e, vi=vi: e.memset(L["vtile"][vi][:, :, 128:130], 1.0), writes=[res(f"vt{vi}")])
        rmsnorm(s, f"mix_norm{layer}")
        subs = subs_of(s)
        proj_fm(s, wqkv, 0, KC, lambda c, sub, bk, br: op(
            "act", lambda e: e.activation(out=Aq[:, c, sub[1]:sub[1] + sub[2]], in_=bk[:, :sub[2]], func=AF.Copy), reads=[br], writes=[aqr]))
        ksr = res("kscr")
        vsr = res("vscr")

        def k_evac(c, sub, bk, br):
            (xcol, lcol, n, kind) = sub
            t, tr = tf.next()
            if evac_flip[0] % 2 == 0:
                op("dve", lambda e: e.tensor_copy(out=t[:, :n], in_=bk[:, :n]), reads=[br], writes=[tr])
            else:
                op("act", lambda e: e.activation(out=t[:, :n], in_=bk[:, :n], func=AF.Copy), reads=[br], writes=[tr])
            evac_flip[0] += 1
            dma("sp", kT_d[j, c, :, xcol:xcol + n], t[:, :n], reads=[tr], writes=[res("kout")])
            dma("pool", ks_d[c, :, xcol:xcol + n], t[:, :n], reads=[tr], writes=[ksr])
        if DBG_ATTN < 1:
            return
        proj_fm(s, wqkv, D, KC, k_evac)
        if DBG_ATTN < 2:
            return
        toks = []
        for (xcol, lcol, n, kind) in subs:
            if kind == "p":
                toks += [(xcol + 128 * i, lcol + 128 * i, 128) for i in range(n // 128)]
            else:
                toks += [(xcol + DSEQ * q, lcol + DSEQ * q, DSEQ) for q in range(NSAMP)]
        for g in range(4):
            wv, wr = wq.request(wqkv[:, 2 * D + 256 * g:2 * D + 256 * g + 256], KC, 256)
            for (xc, lc, m) in toks:
                bk, br = dbank.next()
                for k in range(KC):
                    op("pe", lambda e, k=k, bk=bk: e.matmul(bk[:m, 0:256], lhsT=H[:, k, lc:lc + m], rhs=wv[:, k, :], start=(k == 0), stop=(k == KC - 1)),
                       reads=[wr, hr(lc - lc % 512 if lc < 1024 else 1024)], writes=[br], inc=(k == KC - 1))
                t, tr = tf.next()
                if evac_flip[0] % 2 == 0:
                    op("dve", lambda e, t=t, bk=bk: e.tensor_copy(out=t[:m, 0:256], in_=bk[:m, 0:256]), reads=[br], writes=[tr])
                else:
                    op("act", lambda e, t=t, bk=bk: e.activation(out=t[:m, 0:256], in_=bk[:m, 0:256], func=AF.Copy), reads=[br], writes=[tr])
                evac_flip[0] += 1
                dma("sp", v_d[j, xc:xc + m, 256 * g:256 * g + 256], t[:m, 0:256], reads=[tr], writes=[res("vout")])
                dma("pool", vs_d[xc:xc + m, 256 * g:256 * g + 256], t[:m, 0:256], reads=[tr], writes=[vsr])
        attn_run(s, layer, ksr, vsr)
        proj_fm(s, W["attn_w_o"][j], 0, KC, resid_evac())

    def zero_other_half(c):
        o = (1 - c) * 64
        for (ktv, ktr) in ktl.items:
            op("dve", lambda e, ktv=ktv: e.memset(ktv[o:o + 64, :], 0.0), writes=[ktr])
        op("dve", lambda e: e.memset(kown[o:o + 64, :], 0.0), writes=[res("kown")])

    def attn_run(s, layer, ksr, vsr):
        j = layer // 2
        accbanks = [(banks[i], bank_r[i]) for i in range(4)]
        stb = Rot([(banks[4], bank_r[4]), (banks[5], bank_r[5]), (banks[6], bank_r[6])])
        kor, vor2 = res("kown"), res("vown")
        units = []
        for qi in range(2):
            gq = 2 * s + qi
            for hh in range(KC):
                for c in range(2):
                    units.append(dict(kind="p", hh=hh, c=c, gq=gq, qcol=512 * qi, nq=512, nqt=4, kv={},
                                      tiles=[(kb, kt) for kb in range(gq + 1) for kt in range(4)]))
        if s == 0:
            for q in range(NSAMP):
                for hh in range(KC):
                    for c in range(2):
                        units.append(dict(kind="s", hh=hh, c=c, q=q, gq=-1, qcol=1024 + q * DSEQ, nq=DSEQ, nqt=1, kv={},
                                          tiles=[(kb, kt) for kb in range(2) for kt in range(4)] + [(2, 0)]))
        LA = 2
        flat = [(u, ti) for u in units for ti in range(len(u["tiles"]))]
        kbl = [(u, kb) for u in units for kb in sorted(set(t[0] for t in u["tiles"]))]
        kpos = {(id(u), kb): m for m, (u, kb) in enumerate(kbl)}
        kzero = {}
        k3 = rstd_t[:, 0:256].bitcast(BF16)
        v3 = L["tf"][2][:, 0:260].bitcast(BF16).rearrange("p (t e) -> p t e", t=4)
        ktl3 = Rot(ktl.items + [(k3, res("rstd"))])
        vtl3 = Rot(vtl.items + [(v3, res("tf2"))])
        op("pool", lambda e: e.memset(v3[:, :, 128:130], 1.0), writes=[res("tf2")])

        def emit_load(m):
            u, kb = kbl[m]
            c, hh = u["c"], u["hh"]
            o = (1 - c) * 64
            if u["kind"] == "s" and kb == 2:
                xq = SEQ + u["q"] * DSEQ
                op("pool", lambda e: e.memset(kown[o:o + 64, :], 0.0), writes=[kor])
                dma("sp", kown[c * 64:(c + 1) * 64, :], ks_d[hh, c * 64:(c + 1) * 64, xq:xq + DSEQ], reads=[ksr], writes=[kor])
                dma("sp", vown[:, 0:128], vs_d[xq:xq + DSEQ, 128 * hh:128 * hh + 128], reads=[vsr], writes=[vor2])
                return
            ktv, ktr = ktl3.next()
            vtv, vtr = vtl3.next()
            if kzero.get(ktr.name) != c:
                op("pool", lambda e: e.memset(ktv[o:o + 64, :], 0.0), writes=[ktr])
                kzero[ktr.name] = c
            if u["kind"] == "p":
                dma("sp", ktv[c * 64:(c + 1) * 64, :], ks_d[hh, c * 64:(c + 1) * 64, 512 * kb:512 * kb + 512], reads=[ksr], writes=[ktr])
                dma("sp", vtv[:, :, 0:128], vs_d[512 * kb:512 * kb + 512, 128 * hh:128 * hh + 128].rearrange("(t p) e -> p t e", p=128),
                    reads=[vsr], writes=[vtr])
            else:
                q = u["q"]
                dma("pool", ktv[c * 64:(c + 1) * 64, :], ck_d[j, q, hh, c * 64:(c + 1) * 64, 512 * kb:512 * kb + 512], writes=[ktr])
                dma("pool", vtv[:, :, 0:128], cv_d[j, q, 512 * kb:512 * kb + 512, 128 * hh:128 * hh + 128].rearrange("(t p) e -> p t e", p=128), writes=[vtr])
            u["kv"][kb] = (ktv, ktr, vtv, vtr)

        def emit_st(u, ti):
            kb, kt = u["tiles"][ti]
            c, hh, nq, qcol = u["c"], u["hh"], u["nq"], u["qcol"]
            if ti == 0:
                u["started"] = [False] * u["nqt"]
            m = kpos[(id(u), kb)]
            if m == 0 and kt == 0:
                emit_load(0)
                if len(kbl) > 1:
                    emit_load(1)
            nkt = sum(1 for t_ in u["tiles"] if t_[0] == kb)
            if kt == min(LA, nkt - 1) and m + 2 < len(kbl):
                emit_load(m + 2)
            sbk, sbr = stb.next()
            if u["kind"] == "s" and kb == 2:
                op("pe", lambda e: e.matmul(sbk[:DSEQ, 0:nq], lhsT=kown[:, :], rhs=Aq[:, hh, qcol:qcol + nq], start=True, stop=True),
                   reads=[kor, aqr], writes=[sbr])
                return sbk, sbr
            ktv, ktr, vtv, vtr = u["kv"][kb]
            q0 = kt * 128 if kb == u["gq"] else 0
            op("pe", lambda e: e.matmul(sbk[:, q0:nq], lhsT=ktv[:, kt * 128:(kt + 1) * 128], rhs=Aq[:, hh, qcol + q0:qcol + nq], start=True, stop=True),
               reads=[ktr, aqr], writes=[sbr])
            return sbk, sbr

        def emit_exp_pv(u, ti, sb_):
            sbk, sbr = sb_
            kb, kt = u["tiles"][ti]
            nq, nqt = u["nq"], u["nqt"]
            diag = (kb == u["gq"])
            q0 = kt * 128 if diag else 0
            pt, ptr = tb.next()
            own = (u["kind"] == "s" and kb == 2)
            nk = DSEQ if own else 128
            if not diag:
                op("act", lambda e: e.activation(out=pt[:nk, 0:nq], in_=sbk[:nk, 0:nq], func=AF.Exp, scale=0.125), reads=[sbr], writes=[ptr])
            else:
                op("act", lambda e: e.activation(out=pt[0:64, q0:nq], in_=sbk[0:64, q0:nq], func=AF.Exp, scale=0.125), reads=[sbr], writes=[ptr])
                op("act", lambda e: e.activation(out=pt[64:128, q0 + 64:nq], in_=sbk[64:128, q0 + 64:nq], func=AF.Exp, scale=0.125), reads=[sbr], writes=[ptr])
                op("dve", lambda e: e.memset(pt[64:128, q0:q0 + 64], 0.0), writes=[ptr])
            last_tile = (ti == len(u["tiles"]) - 1)
            if own:
                ab, abr = accbanks[0]
                op("pe", lambda e: e.matmul(ab[:nq, 0:129], lhsT=pt[:DSEQ, 0:nq], rhs=vown[:, 0:129], start=False, stop=True),
                   reads=[ptr, vor2], writes=[abr])
            else:
                vtv, vtr = u["kv"][kb][2], u["kv"][kb][3]
                for qt in (range(kt, nqt) if diag else range(nqt)):
                    ab, abr = accbanks[qt]
                    w = min(128, nq - 128 * qt)
                    last = diag and kt == qt
                    op("pe", lambda e, ab=ab, qt=qt, w=w, st=(not u["started"][qt]), last=last:
                       e.matmul(ab[:w, 0:129], lhsT=pt[:, qt * 128:qt * 128 + w], rhs=vtv[:, kt, 0:129], start=st, stop=last),
                       reads=[ptr, vtr], writes=[abr])
                    u["started"][qt] = True
            if last_tile:
                lcols = [u["qcol"] + 128 * i for i in range(nqt)]
                return evac_acc(accbanks[:nqt], min(128, nq), u["c"], u["hh"], lcols, layer)
            return []

        stq = {}
        pending = []
        n = len(flat)

        def tick(force=False):
            nonlocal pending
            cur, pending = pending, []
            for (cd, f) in cur:
                if cd <= 1 or force:
                    for st in (f() or []):
                        pending.append(st)
                else:
                    pending.append((cd - 1, f))

        for i in range(n + LA):
            if i < n:
                stq[i] = emit_st(*flat[i])
            jx = i - LA
            if jx >= 0:
                tick()
                for st in emit_exp_pv(flat[jx][0], flat[jx][1], stq.pop(jx)):
                    pending.append(st)
        while pending:
            tick(force=True)

    phases = []
    for layer in range(DEPTH):
        for s in range(NSUP):
            phases.append(lambda s=s, layer=layer: ffn(s, layer, "ffn1"))
        for s in range(NSUP):
            if layer % 2 == 0:
                phases.append(lambda s=s, layer=layer: conv_layer(s, layer))
            else:
                phases.append(lambda s=s, layer=layer: attn_layer(s, layer))
        for s in range(NSUP):
            phases.append(lambda s=s, layer=layer: ffn(s, layer, "ffn2"))
    for s in range(NSUP):
        phases.append(lambda s=s: rmsnorm(s, "final_norm", final=True))
    if DBG_PHASES is not None:
        phases = phases[:DBG_PHASES] + phases[-NSUP:]
    for ph_fn in phases:
        ph_fn()
    outs = [res(n) for n in ("yout", "kout", "vout", "cvout")]
    deps = {}
    for r in outs:
        for k, v in r.w.items():
            deps[k] = max(deps.get(k, 0), v)
    ctx._wait("sp", deps)


def _fm(vec):
    v = np.asarray(vec, np.float32).reshape(-1, 128)
    return np.ascontiguousarray(v.T)


def _prep(inp):
    f32 = np.float32
    xp = np.asarray(inp["x_prompt"], f32)
    xs = np.asarray(inp["x_sample"], f32)
    ck = np.asarray(inp["cache_k"], f32)
    cv = np.asarray(inp["cache_v"], f32)
    sc = np.asarray(inp["state_conv"], f32)
    cst = np.zeros((128, NCONST), f32)

    def put(name, arr):
        cst[:, COFF[name]:COFF[name] + arr.shape[1]] = arr
    for i in range(DEPTH):
        put(f"ffn1_norm{i}", _fm(inp["ffn1_norm"][i]))
        put(f"mix_norm{i}", _fm(inp["mix_norm"][i]))
        put(f"ffn2_norm{i}", _fm(inp["ffn2_norm"][i]))
    put("final_norm", _fm(inp["final_norm"]))
    for j in range((DEPTH + 1) // 2):
        put(f"b_pw1{j}", _fm(inp["conv_b_pw1"][j]))
        put(f"b_dw{j}", _fm(inp["conv_b_dw"][j]))
        put(f"ln_g{j}", _fm(inp["conv_ln_g"][j]))
        put(f"ln_b{j}", _fm(inp["conv_ln_b"][j]))
        put(f"b_pw2{j}", _fm(inp["conv_b_pw2"][j]))
        wdw = np.asarray(inp["conv_w_dw"][j], f32)
        put(f"w_dw{j}", np.ascontiguousarray(wdw.T.reshape(8, 128, CW).transpose(1, 0, 2).reshape(128, 8 * CW)))
    for j in range(DEPTH // 2):
        for nm, key in (("lq1", "attn_lambda_q1"), ("lk1", "attn_lambda_k1"), ("lq2", "attn_lambda_q2"), ("lk2", "attn_lambda_k2")):
            put(f"{nm}{j}", np.broadcast_to(np.asarray(inp[key][j], f32)[None, :], (128, 64)))
        put(f"subln{j}", np.broadcast_to(np.asarray(inp["attn_subln"][j], f32)[None, :], (128, 128)))
    put("ident", np.eye(128, dtype=f32))
    wnames = ["ffn1_w_gate", "ffn1_w_up", "ffn1_w_down", "ffn2_w_gate", "ffn2_w_up", "ffn2_w_down",
              "conv_w_pw1", "conv_w_pw2", "attn_w_qkv", "attn_w_o"]
    wts = {n: np.ascontiguousarray(np.asarray(inp[n], f32)) for n in wnames}
    in_maps = []
    for c in range(8):
        p = c // 2
        xpc = xp[p]
        tok = np.concatenate([xpc, xs[2 * c], xs[2 * c + 1]], axis=0)
        xT = np.ascontiguousarray(tok.T.reshape(8, 128, NT).transpose(1, 0, 2))
        ckc = ck[:, 2 * c:2 * c + 2]
        ckc = np.ascontiguousarray(ckc.reshape(NA(), 2, PAST, 8, 128).transpose(0, 1, 3, 4, 2))
        cvc = np.ascontiguousarray(cv[:, 2 * c:2 * c + 2].reshape(NA(), 2, PAST, D))
        scc = sc[:, 2 * c:2 * c + 2]
        scc = np.ascontiguousarray(scc.reshape(NCV(), 2, CS, 8, 128).transpose(0, 1, 4, 3, 2))
        m = {"xT": xT, "ck": ckc, "cv": cvc, "sc": scc, "cst": cst}
        m.update(wts)
        in_maps.append(m)
    return in_maps


def kernel(**inp):
    in_maps = _prep(inp)
    if "nc" not in _NC_CACHE:
        _NC_CACHE["nc"] = build_program()
    nc = _NC_CACHE["nc"]
    res = run_bass_kernel_spmd(nc, in_maps, core_ids=list(range(8)))
    return _assemble(res.results)


def _assemble(R):
    f32 = np.float32
    B, DB = 4, 16
    y_prompt = np.zeros((B, SEQ, D), f32)
    y_sample = np.zeros((DB, DSEQ, D), f32)
    nkp = np.zeros((NA(), B, SEQ, 16, 64), f32)
    nvp = np.zeros((NA(), B, SEQ, 8, 128), f32)
    ncp = np.zeros((NCV(), B, CS, D), f32)
    nks = np.zeros((NA(), DB, DSEQ, 16, 64), f32)
    nvs = np.zeros((NA(), DB, DSEQ, 8, 128), f32)
    ncs = np.zeros((NCV(), DB, CS, D), f32)
    for c in range(8):
        r = R[c]
        y = np.asarray(r["yT"]).reshape(128, 8, NT).transpose(2, 1, 0).reshape(NT, D)
        kk = np.asarray(r["kT_out"]).reshape(NA(), 8, 128, NT).transpose(0, 3, 1, 2).reshape(NA(), NT, D)
        vv = np.asarray(r["v_out"]).reshape(NA(), NT, D)
        co = np.asarray(r["conv_out"]).reshape(NCV(), 3, 128, 8, CS).transpose(0, 1, 4, 3, 2).reshape(NCV(), 3, CS, D)
        if c % 2 == 0:
            p = c // 2
            y_prompt[p] = y[:SEQ]
            nkp[:, p] = kk[:, :SEQ].reshape(NA(), SEQ, 16, 64)
            nvp[:, p] = vv[:, :SEQ].reshape(NA(), SEQ, 8, 128)
            ncp[:, p] = co[:, 0]
        for q in range(2):
            b = 2 * c + q
            sl = slice(SEQ + q * DSEQ, SEQ + (q + 1) * DSEQ)
            y_sample[b] = y[sl]
            nks[:, b] = kk[:, sl].reshape(NA(), DSEQ, 16, 64)
            nvs[:, b] = vv[:, sl].reshape(NA(), DSEQ, 8, 128)
            ncs[:, b] = co[:, 1 + q]
    return (y_prompt, y_sample, nkp, nvp, ncp, nks, nvs, ncs)
```

```python
import math
from contextlib import ExitStack
import numpy as np
import concourse.bass as bass
import concourse.mybir as mybir
from concourse.bass_utils import run_bass_kernel_spmd

F32 = mybir.dt.float32
BF16 = mybir.dt.bfloat16
AF = mybir.ActivationFunctionType
ALU = mybir.AluOpType

D = 1024
KC = 8
DFF = 2816
SEQ = 4096
NSAMP = 2
DSEQ = 32
PAST = 1024
NT = SEQ + NSAMP * DSEQ
NSUP = 4
TS = 1024 + NSAMP * DSEQ
DEPTH = 4
EPS = 1e-6
CW = 31
CS = 30
NSLOT = 6
SLOT = 2048


def NA():
    return DEPTH // 2


def NCV():
    return (DEPTH + 1) // 2


def lambda_init(layer):
    return 0.8 - 0.6 * math.exp(-0.3 * layer)


def const_layout():
    off = {}
    n = 0

    def add(name, w):
        nonlocal n
        off[name] = n
        n += w
    for i in range(DEPTH):
        add(f"ffn1_norm{i}", 8)
        add(f"mix_norm{i}", 8)
        add(f"ffn2_norm{i}", 8)
    add("final_norm", 8)
    for j in range((DEPTH + 1) // 2):
        add(f"b_pw1{j}", 16)
        add(f"b_dw{j}", 8)
        add(f"ln_g{j}", 8)
        add(f"ln_b{j}", 8)
        add(f"b_pw2{j}", 8)
    nres = n
    for j in range((DEPTH + 1) // 2):
        add(f"w_dw{j}", 8 * CW)
    for j in range(DEPTH // 2):
        add(f"subln{j}", 128)
    add("ident", 128)
    for j in range(DEPTH // 2):
        add(f"lq1{j}", 64)
        add(f"lk1{j}", 64)
        add(f"lq2{j}", 64)
        add(f"lk2{j}", 64)
    return off, n, nres


_NC_CACHE = {}
DBG_PHASES = None
DBG_ATTN = 99
FILL_N = 0
COFF, NCONST, NRES = const_layout()


def _configure(seq, depth):
    global SEQ, NT, NSUP, DEPTH, COFF, NCONST, NRES
    SEQ, DEPTH = seq, depth
    NT = SEQ + NSAMP * DSEQ
    NSUP = SEQ // 1024
    COFF, NCONST, NRES = const_layout()
    _NC_CACHE.clear()


class Res:
    __slots__ = ("w", "r", "name", "excl")

    def __init__(self, name="", excl=False):
        self.w = {}
        self.r = {}
        self.name = name
        self.excl = excl


class Ctx:
    def __init__(self, nc, sems, dma_sems, dry):
        self.nc = nc
        self.dry = dry
        self.eng = {"pe": nc.tensor, "act": nc.scalar, "dve": nc.vector,
                    "pool": nc.gpsimd, "sp": nc.sync}
        self.sem = sems
        self.cnt = {k: 0 for k in ("pe", "act", "dve", "pool", "sp")}
        self.waited = {k: {} for k in self.eng}
        self.dma_sems = dma_sems
        self.dma_n = {"sp": 0, "pool": 0}
        self.semobj = {}
        self.ninst = 0

    def _wait(self, E, deps):
        eng = self.eng[E]
        wd = self.waited[E]
        for key, v in deps.items():
            if wd.get(key, 0) < v:
                wd[key] = v
                if not self.dry:
                    eng.wait_ge(self.semobj[key], v)

    def _deps(self, reads, writes):
        deps = {}
        for r in reads:
            for k, v in r.w.items():
                if deps.get(k, 0) < v:
                    deps[k] = v
        for w in writes:
            for k, v in w.w.items():
                if deps.get(k, 0) < v:
                    deps[k] = v
            for k, v in w.r.items():
                if deps.get(k, 0) < v:
                    deps[k] = v
        return deps

    def _mark(self, key, v, reads, writes):
        for r in reads:
            if r.r.get(key, 0) < v:
                r.r[key] = v
        for w in writes:
            if w.w.get(key, 0) < v:
                w.w[key] = v

    def op(self, E, fn, reads=(), writes=(), inc=True):
        ex = [r for r in reads if r.excl]
        if ex:
            writes = list(writes) + ex
        deps = self._deps(reads, writes)
        if E == "pe":
            deps.pop("pe", None)
        self._wait(E, deps)
        self.ninst += 1
        key = E
        inc = True
        if inc:
            self.cnt[E] += 1
            v = self.cnt[E]
            if not self.dry:
                fn(self.eng[E]).then_inc(self.sem[E], 1)
        else:
            v = self.cnt[E] + 1
            if not self.dry:
                fn(self.eng[E])
        self._mark(key, v, reads, writes)

    def dma(self, Q, out, in_, reads=(), writes=()):
        deps = self._deps(reads, writes)
        i = self.dma_n[Q]
        self.dma_n[Q] += 1
        nsem = len(self.dma_sems[Q])
        key = f"d{Q}{i % nsem}"
        val = 16 * (i // nsem + 1)
        if i >= nsem:
            deps[key] = max(deps.get(key, 0), val - 16)
        self._wait(Q, deps)
        self.ninst += 1
        if not self.dry:
            self.eng[Q].dma_start(out=out, in_=in_).then_inc(self.semobj[key], 16)
        self._mark(key, val, reads, writes)
        return key, val


class WQ:
    def __init__(self, ctx, wbuf, plan):
        self.ctx = ctx
        self.wbuf = wbuf
        self.plan = plan
        self.rec = []
        self.i = 0
        self.issued = 0
        self.res = [Res(f"slot{i}") for i in range(NSLOT)]
        self.PF = NSLOT - 4

    def view(self, idx, kch, ncols):
        s = idx % NSLOT
        ap = self.wbuf[:, s * SLOT: s * SLOT + kch * ncols]
        return ap.rearrange("p (k n) -> p k n", k=kch), self.res[s]

    def _issue(self, idx):
        spec = self.plan[idx]
        if spec is None:
            return
        src, kch, ncols = spec
        v, r = self.view(idx, kch, ncols)
        self.ctx.dma("pool", v, src.rearrange("(k p) n -> p k n", p=128), writes=[r])

    def request(self, src, kch, ncols):
        idx = self.i
        self.i += 1
        if self.plan is None:
            self.rec.append(None if src is None else (src, kch, ncols))
        else:
            hi = min(len(self.plan), idx + self.PF + 1)
            while self.issued < hi:
                self._issue(self.issued)
                self.issued += 1
        return self.view(idx, kch, ncols)


class Rot:
    def __init__(self, items):
        self.items = items
        self.i = 0

    def next(self):
        it = self.items[self.i % len(self.items)]
        self.i += 1
        return it


def build_program():
    nc = bass.Bass("TRN2", target_bir_lowering=False, dynamic_dma_scratch_size=None)
    dt = nc.dram_tensor
    xT_d = dt("xT", [128, KC, NT], F32, kind="ExternalInput").ap()
    ck_d = dt("ck", [NA(), NSAMP, 8, 128, PAST], F32, kind="ExternalInput").ap()
    cv_d = dt("cv", [NA(), NSAMP, PAST, D], F32, kind="ExternalInput").ap()
    sc_d = dt("sc", [NCV(), NSAMP, 128, KC, CS], F32, kind="ExternalInput").ap()
    cst_d = dt("cst", [128, NCONST], F32, kind="ExternalInput").ap()
    W = {}
    for nm, shp in (("ffn1_w_gate", [DEPTH, D, DFF]), ("ffn1_w_up", [DEPTH, D, DFF]),
                    ("ffn1_w_down", [DEPTH, DFF, D]), ("ffn2_w_gate", [DEPTH, D, DFF]),
                    ("ffn2_w_up", [DEPTH, D, DFF]), ("ffn2_w_down", [DEPTH, DFF, D]),
                    ("conv_w_pw1", [NCV(), D, 2 * D]), ("conv_w_pw2", [NCV(), D, D]),
                    ("attn_w_qkv", [NA(), D, 3 * D]), ("attn_w_o", [NA(), D, D])):
        W[nm] = dt(nm, shp, F32, kind="ExternalInput").ap()
    yT_d = dt("yT", [128, KC, NT], F32, kind="ExternalOutput").ap()
    kT_d = dt("kT_out", [NA(), 8, 128, NT], F32, kind="ExternalOutput").ap()
    v_d = dt("v_out", [NA(), NT, D], F32, kind="ExternalOutput").ap()
    cv_o = dt("conv_out", [NCV(), 3, 128, KC, CS], F32, kind="ExternalOutput").ap()
    ks_d = dt("k_scr", [8, 128, NT], BF16, kind="Internal").ap()
    vs_d = dt("v_scr", [NT, D], BF16, kind="Internal").ap()

    with ExitStack() as es:
        def sb(name, shape, dtype):
            return es.enter_context(nc.sbuf_tensor(name, shape, dtype))

        def ps(name, shape, dtype):
            return es.enter_context(nc.psum_tensor(name, shape, dtype))

        x = sb("x_sb", [128, KC, NT], F32)
        H = sb("H", [128, KC, TS], BF16)
        A = sb("A", [128, KC * TS], BF16)
        wbuf = sb("wbuf", [128, NSLOT * SLOT], BF16)
        cst = sb("cst_sb", [128, NRES], F32)
        ident = sb("ident", [128, 128], BF16)
        ones = sb("ones", [128, 128], BF16)
        tf = [sb(f"tf{i}", [128, 512], F32) for i in range(3)]
        tb = [sb(f"tb{i}", [128, 512], BF16) for i in range(2)]
        rstd_t = sb("rstd", [128, 512], F32)
        mean_t = sb("mean", [128, 512], F32)
        o0 = mean_t[:, :].rearrange("p (q e) -> p q e", q=4)
        ph = sb("ph", [128, 1464], F32)
        uo = ph[:, 0:720].rearrange("p (k s n) -> p k s n", k=KC, s=3)
        wdw_v = ph[:, 720:968]
        Us = ph[:, 968:1464].bitcast(BF16).rearrange("p (k s n) -> p k s n", k=KC, s=NSAMP)
        ktile = [ph[:, 256 * i:256 * (i + 1)].bitcast(BF16) for i in range(2)]
        vtile = [ph[:, 512 + 260 * i:512 + 260 * (i + 1)].bitcast(BF16).rearrange("p (t e) -> p t e", t=4) for i in range(2)]
        subln_v = ph[:, 1032:1160]
        onb = ph[:, 1160:1416].bitcast(BF16)
        kown = sb("kown", [128, 32], BF16)
        vown = sb("vown", [32, 130], BF16)
        small = sb("small", [128, 64], F32)
        banks = [ps(f"bk{i}", [128, 512], F32) for i in range(7)]
        bankT = ps("bkT", [128, 1024], BF16)

        sems = {k: es.enter_context(nc.semaphore(f"s_{k}")) for k in ("pe", "act", "dve", "pool", "sp")}
        dma_sems = {q: [es.enter_context(nc.semaphore(f"d{q}{i}")) for i in range(8)] for q in ("sp", "pool")}

        plan = None
        for dry in (True, False):
            ctx = Ctx(nc, sems, dma_sems, dry)
            for k, s in sems.items():
                ctx.semobj[k] = s
            for q in ("sp", "pool"):
                for i, s in enumerate(dma_sems[q]):
                    ctx.semobj[f"d{q}{i}"] = s
            wq = WQ(ctx, wbuf, plan)
            emit(ctx, wq, locals())
            if dry:
                plan = wq.rec
    return nc


def emit(ctx, wq, L):
    x, H, A, cst = L["x"], L["H"], L["A"], L["cst"]
    ident, ones = L["ident"], L["ones"]
    banks, bankT = L["banks"], L["bankT"]
    W = L["W"]
    xT_d, ck_d, cv_d, sc_d, cst_d = L["xT_d"], L["ck_d"], L["cv_d"], L["sc_d"], L["cst_d"]
    yT_d, kT_d, v_d, cv_o, ks_d, vs_d = L["yT_d"], L["kT_d"], L["v_d"], L["cv_o"], L["ks_d"], L["vs_d"]
    o0, small, uo, kown, vown = L["o0"], L["small"], L["uo"], L["kown"], L["vown"]
    rstd_t, mean_t = L["rstd_t"], L["mean_t"]
    wdw_v, Us, subln_v = L["wdw_v"], L["Us"], L["subln_v"]
    op, dma = ctx.op, ctx.dma

    R = {}

    def res(name):
        if name not in R:
            R[name] = Res(name)
        return R[name]
    bank_r = [res(f"bank{i}") for i in range(7)]
    bankT_r = res("bankT")
    for b_ in bank_r + [bankT_r]:
        b_.excl = True
    tf = Rot(list(zip(L["tf"], [res(f"tf{i}") for i in range(3)])))
    tb = Rot(list(zip(L["tb"], [res(f"tb{i}") for i in range(2)])))
    ktl = Rot(list(zip(L["ktile"], [res(f"kt{i}") for i in range(2)])))
    vtl = Rot(list(zip(L["vtile"], [res(f"vt{i}") for i in range(2)])))
    gbank = Rot([(banks[0], bank_r[0]), (banks[1], bank_r[1])])
    ubank = Rot([(banks[2], bank_r[2]), (banks[3], bank_r[3])])
    dbank = Rot([(banks[4], bank_r[4]), (banks[5], bank_r[5])])
    sbank = (banks[6], bank_r[6])
    cres = res("cst")
    tf2rot = Rot(tf.items[0:2])
    evac_ctr = [0]
    eps_c = small[:, 16:17]
    onb = Rot([(L["onb"][:, 128 * i:128 * (i + 1)], res(f"onb{i}")) for i in range(4)])
    idres = res("ident")

    def cc(name, k=0, w=1):
        o = COFF[name] + k
        return cst[:, o:o + w]

    def merge(dst, srcs):
        for sr in srcs:
            for k, v in sr.w.items():
                if dst.w.get(k, 0) < v:
                    dst.w[k] = v
            for k, v in sr.r.items():
                if dst.r.get(k, 0) < v:
                    dst.r[k] = v

    def a_alias(kind):
        allr = [res("a0"), res("a512"), res("a1024"), res("Up"), res("Aq")]
        tgt = {"ffn": allr[0:3], "conv": [allr[3]], "attn": [allr[4]]}[kind]
        for t in tgt:
            merge(t, allr)

    def ph_alias(kind):
        allr = [res("uo"), res("wdw"), res("Us"), res("kt0"), res("kt1"), res("vt0"), res("vt1"), res("subln")] + [res(f"onb{i}") for i in range(4)]
        tgt = allr[0:3] if kind == "conv" else allr[3:]
        for t in tgt:
            merge(t, allr)

    def subs_of(s):
        out = [(1024 * s, 0, 512, "p"), (1024 * s + 512, 512, 512, "p")]
        if s == 0:
            out.append((SEQ, 1024, NSAMP * DSEQ, "s"))
        return out

    def xr(xcol):
        return res(f"x{xcol}")

    def hr(lcol):
        return res(f"h{lcol}")

    def ar(lcol):
        return res(f"a{lcol}")

    dma("sp", cst[:, :], cst_d[:, 0:NRES], writes=[cres])
    for k in range(KC):
        dma("sp", x[:, k, :], xT_d[:, k, :], writes=[xr(c) for s in range(NSUP) for (c, _, _, _) in subs_of(s)])
    op("pool", lambda e: e.memset(ones[:, :], 1.0), writes=[idres])
    op("pool", lambda e: e.memset(small[:, 16:17], EPS), writes=[res("small")])
    it, itr = tf.next()
    dma("sp", it[:, 0:128], cst_d[:, COFF["ident"]:COFF["ident"] + 128], writes=[itr])
    op("dve", lambda e: e.tensor_copy(out=ident[:, :], in_=it[:, 0:128]), reads=[itr], writes=[idres])
    vor = res("vown")
    op("pool", lambda e: e.memset(vown[:, 128:130], 1.0), writes=[vor])

    def rmsnorm(s, gname, final=False):
        for (xcol, lcol, n, kind) in subs_of(s):
            sbk, sbr = sbank
            for k in range(KC):
                t, tr = tb.next()
                op("act", lambda e, t=t, k=k: e.activation(out=t[:, :n], in_=x[:, k, xcol:xcol + n], func=AF.Square),
                   reads=[xr(xcol)], writes=[tr])
                op("pe", lambda e, t=t, k=k: e.matmul(sbk[:, :n], lhsT=ones[:, :], rhs=t[:, :n], start=(k == 0), stop=(k == KC - 1)),
                   reads=[tr, idres], writes=[sbr], inc=(k == KC - 1))
            rr = res("rstd")
            op("act", lambda e: e.activation(out=rstd_t[:, :n], in_=sbk[:, :n], func=AF.Sqrt, scale=1.0 / D, bias=EPS),
               reads=[sbr], writes=[rr])
            op("dve", lambda e: e.reciprocal(out=rstd_t[:, :n], in_=rstd_t[:, :n]), reads=[rr], writes=[rr])
            for k in range(KC):
                if not final:
                    op("dve", lambda e, k=k: e.scalar_tensor_tensor(out=H[:, k, lcol:lcol + n], in0=x[:, k, xcol:xcol + n],
                                                                    scalar=cc(gname, k), in1=rstd_t[:, :n], op0=ALU.mult, op1=ALU.mult),
                       reads=[xr(xcol), rr, cres], writes=[hr(lcol)])
                else:
                    t, tr = tf.next()
                    op("dve", lambda e, k=k, t=t: e.scalar_tensor_tensor(out=t[:, :n], in0=x[:, k, xcol:xcol + n],
                                                                         scalar=cc(gname, k), in1=rstd_t[:, :n], op0=ALU.mult, op1=ALU.mult),
                       reads=[xr(xcol), rr, cres], writes=[tr])
                    dma("sp", yT_d[:, k, xcol:xcol + n], t[:, :n], reads=[tr], writes=[res("yout")])

    def proj_fm(s, wsrc, col0, nchunks, evac):
        for g in range(nchunks // 2):
            wv, wr = wq.request(wsrc[:, col0 + 256 * g: col0 + 256 * g + 256], KC, 256)
            for sub in subs_of(s):
                (xcol, lcol, n, kind) = sub
                for jj in range(2):
                    bk, br = dbank.next()
                    for k in range(KC):
                        op("pe", lambda e, k=k, bk=bk, jj=jj: e.matmul(bk[:, :n], lhsT=wv[:, k, jj * 128:(jj + 1) * 128], rhs=H[:, k, lcol:lcol + n],
                                                                       start=(k == 0), stop=(k == KC - 1)),
                           reads=[wr, hr(lcol)], writes=[br], inc=(k == KC - 1))
                    evac(2 * g + jj, sub, bk, br)

    def resid_evac(bias_name=None, scale=1.0):
        def f(c, sub, bk, br):
            (xcol, lcol, n, kind) = sub
            if bias_name is None:
                op("dve", lambda e: e.scalar_tensor_tensor(out=x[:, c, xcol:xcol + n], in0=bk[:, :n], scalar=scale,
                                                           in1=x[:, c, xcol:xcol + n], op0=ALU.mult, op1=ALU.add),
                   reads=[br, xr(xcol)], writes=[xr(xcol)])
            else:
                op("dve", lambda e: e.scalar_tensor_tensor(out=x[:, c, xcol:xcol + n], in0=bk[:, :n], scalar=cc(bias_name, c),
                                                           in1=x[:, c, xcol:xcol + n], op0=ALU.add, op1=ALU.add),
                   reads=[br, xr(xcol), cres], writes=[xr(xcol)])
        return f

    def ffn(s, layer, which):
        a_alias("ffn")
        rmsnorm(s, f"{which}_norm{layer}")
        wg, wu, wd = W[f"{which}_w_gate"][layer], W[f"{which}_w_up"][layer], W[f"{which}_w_down"][layer]
        Av = A[:, 0:2 * TS].rearrange("p (j n) -> p j n", j=2)
        dq = []

        def emit_d(n_units):
            for _ in range(min(n_units, len(dq))):
                (dv, dr, xcol, lcol, n, d) = dq.pop(0)
                bk, br = dbank.next()
                for jj in range(2):
                    op("pe", lambda e, jj=jj: e.matmul(bk[:, :n], lhsT=dv[:, jj, d * 128:(d + 1) * 128], rhs=Av[:, jj, lcol:lcol + n],
                                                       start=(jj == 0), stop=(jj == 1)),
                       reads=[dr, ar(lcol)], writes=[br])
                op("dve", lambda e: e.scalar_tensor_tensor(out=x[:, d, xcol:xcol + n], in0=bk[:, :n], scalar=0.5,
                                                           in1=x[:, d, xcol:xcol + n], op0=ALU.mult, op1=ALU.add),
                   reads=[br, xr(xcol)], writes=[xr(xcol)])

        for g in range(DFF // 256):
            gv, gr = wq.request(wg[:, 256 * g:256 * g + 256], KC, 256)
            uv, ur = wq.request(wu[:, 256 * g:256 * g + 256], KC, 256)
            dv, dr = wq.request(wd[256 * g:256 * g + 256, :], 2, D)
            for (xcol, lcol, n, kind) in subs_of(s):
                pend_here = sum(1 for q_ in dq if q_[3] == lcol)
                if pend_here:
                    emit_d(len(dq))
                for jj in range(2):
                    gb, gbr = gbank.next()
                    ub, ubr = ubank.next()
                    for k in range(KC):
                        op("pe", lambda e, k=k: e.matmul(gb[:, :n], lhsT=gv[:, k, jj * 128:(jj + 1) * 128], rhs=H[:, k, lcol:lcol + n],
                                                         start=(k == 0), stop=(k == KC - 1)),
                           reads=[gr, hr(lcol)], writes=[gbr])
                    emit_d(2)
                    for k in range(KC):
                        op("pe", lambda e, k=k: e.matmul(ub[:, :n], lhsT=uv[:, k, jj * 128:(jj + 1) * 128], rhs=H[:, k, lcol:lcol + n],
                                                         start=(k == 0), stop=(k == KC - 1)),
                           reads=[ur, hr(lcol)], writes=[ubr])
                    t, tr = tf.next()
                    op("act", lambda e: e.activation(out=t[:, :n], in_=gb[:, :n], func=AF.Silu), reads=[gbr], writes=[tr])
                    op("dve", lambda e: e.tensor_tensor(out=Av[:, jj, lcol:lcol + n], in0=t[:, :n], in1=ub[:, :n], op=ALU.mult),
                       reads=[tr, ubr], writes=[ar(lcol)])
                    emit_d(2)
                for d in range(KC):
                    dq.append((dv, dr, xcol, lcol, n, d))
            if len(dq) > 2 * KC:
                emit_d(len(dq) - 2 * KC)
        emit_d(len(dq))

    UPW = 30 + 1024
    Up = A[:, 0:KC * UPW].rearrange("p (k n) -> p k n", k=KC)
    upr = res("Up")
    usr = res("Us")
    wdr = res("wdw")

    def conv_layer(s, layer):
        j = layer // 2
        a_alias("conv")
        ph_alias("conv")
        rmsnorm(s, f"mix_norm{layer}")
        if s == 0:
            dma("sp", wdw_v, cst_d[:, COFF[f"w_dw{j}"]:COFF[f"w_dw{j}"] + 8 * CW], writes=[wdr])
            op("pool", lambda e: e.memset(Up[:, :, 0:CS], 0.0), writes=[upr])
            for q in range(NSAMP):
                dma("pool", Us[:, :, q, 0:CS], sc_d[j, q], writes=[usr])
        else:
            op("dve", lambda e: e.tensor_copy(out=Up[:, :, 0:CS], in_=Up[:, :, 1024:1024 + CS]), reads=[upr], writes=[upr])
        w1 = W["conv_w_pw1"][j]
        uor = res("uo")
        for g in range(4):
            vv, vr = wq.request(w1[:, 256 * g:256 * g + 256], KC, 256)
            gv, gr = wq.request(w1[:, D + 256 * g:D + 256 * g + 256], KC, 256)
            for (xcol, lcol, n, kind) in subs_of(s):
                for jj in range(2):
                    c = 2 * g + jj
                    vb, vbr = gbank.next()
                    gb, gbr = ubank.next()
                    for k in range(KC):
                        op("pe", lambda e, k=k, vb=vb, jj=jj: e.matmul(vb[:, :n], lhsT=vv[:, k, jj * 128:(jj + 1) * 128], rhs=H[:, k, lcol:lcol + n],
                                                                       start=(k == 0), stop=(k == KC - 1)),
                           reads=[vr, hr(lcol)], writes=[vbr], inc=(k == KC - 1))
                    for k in range(KC):
                        op("pe", lambda e, k=k, gb=gb, jj=jj: e.matmul(gb[:, :n], lhsT=gv[:, k, jj * 128:(jj + 1) * 128], rhs=H[:, k, lcol:lcol + n],
                                                                       start=(k == 0), stop=(k == KC - 1)),
                           reads=[gr, hr(lcol)], writes=[gbr], inc=(k == KC - 1))
                    t, tr = tf.next()
                    op("act", lambda e, t=t, gb=gb, c=c: e.activation(out=t[:, :n], in_=gb[:, :n], func=AF.Sigmoid, bias=cc(f"b_pw1{j}", 8 + c)),
                       reads=[gbr, cres], writes=[tr])
                    if kind == "p":
                        op("dve", lambda e, t=t, vb=vb, c=c: e.scalar_tensor_tensor(out=Up[:, c, CS + lcol:CS + lcol + n], in0=vb[:, :n], scalar=cc(f"b_pw1{j}", c),
                                                                                    in1=t[:, :n], op0=ALU.add, op1=ALU.mult),
                           reads=[tr, vbr, cres], writes=[upr])
                        if s == NSUP - 1 and lcol == 512:
                            op("dve", lambda e, t=t, vb=vb, c=c: e.scalar_tensor_tensor(out=uo[:, c, 0, :], in0=vb[:, 512 - CS:512], scalar=cc(f"b_pw1{j}", c),
                                                                                        in1=t[:, 512 - CS:512], op0=ALU.add, op1=ALU.mult),
                               reads=[tr, vbr, cres], writes=[uor])
                    else:
                        for q in range(NSAMP):
                            op("dve", lambda e, t=t, vb=vb, c=c, q=q: e.scalar_tensor_tensor(out=Us[:, c, q, CS:CS + DSEQ], in0=vb[:, q * DSEQ:(q + 1) * DSEQ], scalar=cc(f"b_pw1{j}", c),
                                                                                             in1=t[:, q * DSEQ:(q + 1) * DSEQ], op0=ALU.add, op1=ALU.mult),
                               reads=[tr, vbr, cres], writes=[usr])
                            op("dve", lambda e, t=t, vb=vb, c=c, q=q: e.scalar_tensor_tensor(out=uo[:, c, 1 + q, :], in0=vb[:, q * DSEQ + 2:(q + 1) * DSEQ], scalar=cc(f"b_pw1{j}", c),
                                                                                             in1=t[:, q * DSEQ + 2:(q + 1) * DSEQ], op0=ALU.add, op1=ALU.mult),
                               reads=[tr, vbr, cres], writes=[uor])
        if s == NSUP - 1:
            dma("sp", cv_o[j, 0], uo[:, :, 0, :], reads=[uor], writes=[res("cvout")])
        if s == 0:
            for q in range(NSAMP):
                dma("sp", cv_o[j, 1 + q], uo[:, :, 1 + q, :], reads=[uor], writes=[res("cvout")])
        for c in range(KC):
            dgs = []
            for (t0, t1) in ((0, 16), (16, CW)):
                dgv, dgr = wq.request(None, 16, 128)
                for tap in range(t0, t1):
                    op("dve", lambda e, tap=tap, dgv=dgv, t0=t0: e.tensor_scalar(out=dgv[:, tap - t0, :], in0=ident[:, :], scalar1=wdw_v[:, c * CW + tap:c * CW + tap + 1],
                                                                                 scalar2=None, op0=ALU.mult),
                       reads=[idres, wdr], writes=[dgr])
                dgs.append((dgv, dgr, t0, t1))
            for (xcol, lcol, n, kind) in subs_of(s):
                units = [(None, lcol, n)] if kind == "p" else [(q, lcol + q * DSEQ, DSEQ) for q in range(NSAMP)]
                for (q, lc, nn) in units:
                    bk, br = dbank.next()
                    for (dgv, dgr, t0, t1) in dgs:
                        for tap in range(t0, t1):
                            rhs = Up[:, c, lc + tap:lc + tap + nn] if q is None else Us[:, c, q, tap:tap + nn]
                            op("pe", lambda e, tap=tap, dgv=dgv, t0=t0, rhs=rhs, bk=bk: e.matmul(bk[:, :nn], lhsT=dgv[:, tap - t0, :], rhs=rhs,
                                                                                                 start=(tap == 0), stop=(tap == CW - 1)),
                               reads=[dgr, upr if q is None else usr], writes=[br], inc=(tap == CW - 1))
                    op("act", lambda e, bk=bk, lc=lc, nn=nn: e.activation(out=H[:, c, lc:lc + nn], in_=bk[:, :nn], func=AF.Identity, bias=cc(f"b_dw{j}", c)),
                       reads=[br, cres], writes=[hr(lcol)])
        b1, b1r = gbank.next()
        b2, b2r = ubank.next()
        for (xcol, lcol, n, kind) in subs_of(s):
            for k in range(KC):
                t, tr = tb.next()
                op("act", lambda e, t=t, k=k: e.activation(out=t[:, :n], in_=H[:, k, lcol:lcol + n], func=AF.Square), reads=[hr(lcol)], writes=[tr])
                op("pe", lambda e, k=k: e.matmul(b1[:, :n], lhsT=ones[:, :], rhs=H[:, k, lcol:lcol + n], start=(k == 0), stop=(k == KC - 1)),
                   reads=[hr(lcol), idres], writes=[b1r], inc=(k == KC - 1))
                op("pe", lambda e, k=k, t=t: e.matmul(b2[:, :n], lhsT=ones[:, :], rhs=t[:, :n], start=(k == 0), stop=(k == KC - 1)),
                   reads=[tr, idres], writes=[b2r], inc=(k == KC - 1))
            mr, rr = res("mean"), res("rstd")
            t, tr = tf.next()
            op("dve", lambda e: e.tensor_scalar(out=mean_t[:, :n], in0=b1[:, :n], scalar1=1.0 / D, scalar2=None, op0=ALU.mult), reads=[b1r], writes=[mr])
            op("dve", lambda e, t=t: e.tensor_tensor(out=t[:, :n], in0=mean_t[:, :n], in1=mean_t[:, :n], op=ALU.mult), reads=[mr], writes=[tr])
            op("dve", lambda e, t=t: e.scalar_tensor_tensor(out=rstd_t[:, :n], in0=b2[:, :n], scalar=1.0 / D, in1=t[:, :n], op0=ALU.mult, op1=ALU.subtract),
               reads=[b2r, tr], writes=[rr])
            op("act", lambda e: e.activation(out=rstd_t[:, :n], in_=rstd_t[:, :n], func=AF.Sqrt, bias=EPS), reads=[rr], writes=[rr])
            op("dve", lambda e: e.reciprocal(out=rstd_t[:, :n], in_=rstd_t[:, :n]), reads=[rr], writes=[rr])
            for k in range(KC):
                t, tr = tf.next()
                t2, tr2 = tf.next()
                op("dve", lambda e, t=t, k=k: e.tensor_tensor(out=t[:, :n], in0=H[:, k, lcol:lcol + n], in1=mean_t[:, :n], op=ALU.subtract),
                   reads=[hr(lcol), mr], writes=[tr])
                op("dve", lambda e, t=t, t2=t2: e.tensor_tensor(out=t2[:, :n], in0=t[:, :n], in1=rstd_t[:, :n], op=ALU.mult), reads=[tr, rr], writes=[tr2])
                op("act", lambda e, t2=t2, k=k: e.activation(out=H[:, k, lcol:lcol + n], in_=t2[:, :n], func=AF.Silu, scale=cc(f"ln_g{j}", k), bias=cc(f"ln_b{j}", k)),
                   reads=[tr2, cres], writes=[hr(lcol)])
        proj_fm(s, W["conv_w_pw2"][j], 0, KC, resid_evac(bias_name=f"b_pw2{j}"))

    Aq = A[:, 0:KC * TS].rearrange("p (h n) -> p h n", h=KC)
    aqr = res("Aq")
    smr = res("small")

    def attn_lambda(layer):
        j = layer // 2
        lt, ltr = tf.next()
        dma("sp", lt[:, 0:256], cst_d[:, COFF[f"lq1{j}"]:COFF[f"lq1{j}"] + 256], writes=[ltr])
        for i in range(2):
            t, tr = tf.next()
            op("dve", lambda e, t=t: e.tensor_tensor(out=t[:, :64], in0=lt[:, 128 * i:128 * i + 64], in1=lt[:, 128 * i + 64:128 * i + 128], op=ALU.mult), reads=[ltr], writes=[tr])
            op("dve", lambda e, t=t, i=i: e.reduce_sum(out=small[:, i:i + 1], in_=t[:, :64], axis=mybir.AxisListType.X), reads=[tr], writes=[smr])
            op("act", lambda e, i=i: e.activation(out=small[:, 2 + i:3 + i], in_=small[:, i:i + 1], func=AF.Exp), reads=[smr], writes=[smr])
        li = lambda_init(layer)
        op("dve", lambda e: e.scalar_tensor_tensor(out=small[:, 4:5], in0=small[:, 3:4], scalar=-li, in1=small[:, 2:3], op0=ALU.add, op1=ALU.subtract),
           reads=[smr], writes=[smr])

    def evac_acc(accs, nq, cmap, hh, lcols, layer):
        li = lambda_init(layer)
        o0r = res("mean")
        nqt = len(accs)
        W_ = 128 * nqt
        for qi, (bk, br) in enumerate(accs):
            op("dve", lambda e, bk=bk, qi=qi: e.reciprocal(out=small[:nq, 20 + qi:21 + qi], in_=bk[:nq, 128:129]), reads=[br], writes=[smr])
        if cmap == 0:
            for qi, (bk, br) in enumerate(accs):
                op("dve", lambda e, bk=bk, qi=qi: e.tensor_scalar(out=o0[:nq, qi, :], in0=bk[:nq, 0:128], scalar1=small[:nq, 20 + qi:21 + qi], scalar2=None, op0=ALU.mult),
                   reads=[br, smr], writes=[o0r])
            return []
        ta, tar = tf2rot.next()
        tb_, tbr_ = tf2rot.next()
        for qi, (bk, br) in enumerate(accs):
            op("dve", lambda e, bk=bk, qi=qi: e.tensor_scalar(out=ta[:nq, 128 * qi:128 * qi + 128], in0=bk[:nq, 0:128], scalar1=small[:nq, 20 + qi:21 + qi],
                                                              scalar2=small[:nq, 4:5], op0=ALU.mult, op1=ALU.mult),
               reads=[br, smr], writes=[tar])
        o0f = mean_t[:nq, 0:W_]
        op("dve", lambda e: e.tensor_tensor(out=ta[:nq, 0:W_], in0=ta[:nq, 0:W_], in1=o0f, op=ALU.add), reads=[tar, o0r], writes=[tar])
        op("dve", lambda e: e.tensor_tensor(out=tb_[:nq, 0:W_], in0=ta[:nq, 0:W_], in1=ta[:nq, 0:W_], op=ALU.mult), reads=[tar], writes=[tbr_])
        ssum = small[:nq, 48 + 4 * (evac_ctr[0] % 2):48 + 4 * (evac_ctr[0] % 2) + nqt]
        op("dve", lambda e: e.reduce_sum(out=ssum, in_=tb_[:nq, 0:W_].rearrange("p (q e) -> p q e", q=nqt), axis=mybir.AxisListType.X),
           reads=[tbr_], writes=[smr])
        ob = L["onb"]
        obr = res("onb0")
        lcol0 = lcols[0]
        sm_lo = 32 + 8 * (evac_ctr[0] % 2)
        evac_ctr[0] += 1

        def stage3():
            for qi in range(nqt):
                op("pe", lambda e, qi=qi: e.transpose(bankT[:, nq * qi:nq * qi + nq], ob[:nq, 128 * qi:128 * qi + 128], ident[:nq, :nq]), reads=[obr, idres], writes=[bankT_r])
            op("dve", lambda e: e.tensor_copy(out=H[:, hh, lcol0:lcol0 + nq * nqt], in_=bankT[:, 0:nq * nqt]), reads=[bankT_r],
               writes=[hr(lcol0 - lcol0 % 512 if lcol0 < 1024 else 1024)])
            return []

        def stage2():
            op("act", lambda e: e.activation(out=small[:nq, sm_lo:sm_lo + nqt], in_=small[:nq, 24:24 + nqt] if False else ssum, func=AF.Ln, scale=1.0 / 128, bias=eps_c[:nq, :]),
               reads=[smr], writes=[smr])
            op("act", lambda e: e.activation(out=small[:nq, sm_lo + 4:sm_lo + 4 + nqt], in_=small[:nq, sm_lo:sm_lo + nqt], func=AF.Exp, scale=-0.5), reads=[smr], writes=[smr])
            op("dve", lambda e: e.tensor_scalar(out=small[:nq, sm_lo + 4:sm_lo + 4 + nqt], in0=small[:nq, sm_lo + 4:sm_lo + 4 + nqt], scalar1=(1.0 - li), scalar2=None, op0=ALU.mult),
               reads=[smr], writes=[smr])
            for qi in range(nqt):
                op("dve", lambda e, qi=qi: e.scalar_tensor_tensor(out=ob[:nq, 128 * qi:128 * qi + 128], in0=ta[:nq, 128 * qi:128 * qi + 128], scalar=small[:nq, sm_lo + 4 + qi:sm_lo + 5 + qi],
                                                                  in1=subln_v[:nq, :], op0=ALU.mult, op1=ALU.mult),
                   reads=[tar, smr, res("subln")], writes=[obr])
            return [(4, stage3)]
        return [(5, stage2)]

    def attn_layer(s, layer):
        j = layer // 2
        wqkv = W["attn_w_qkv"][j]
        a_alias("attn")
        ph_alias("attn")
        if s == 0:
            attn_lambda(layer)
            dma("sp", subln_v, cst_d[:, COFF[f"subln{j}"]:COFF[f"subln{j}"] + 128], writes=[res("subln")])
            for vi in range(2):
                op("pool", lambda e, vi=vi: e.memset(L["vtile"][vi][:, :, 128:130], 1.0), writes=[res(f"vt{vi}")])
        rmsnorm(s, f"mix_norm{layer}")
        subs = subs_of(s)
        proj_fm(s, wqkv, 0, KC, lambda c, sub, bk, br: op(
            "act", lambda e: e.activation(out=Aq[:, c, sub[1]:sub[1] + sub[2]], in_=bk[:, :sub[2]], func=AF.Copy), reads=[br], writes=[aqr]))
        ksr = res("kscr")
        vsr = res("vscr")

        def k_evac(c, sub, bk, br):
            (xcol, lcol, n, kind) = sub
            t, tr = tf.next()
            t2, tr2 = tb.next()
            op("dve", lambda e: e.tensor_copy(out=t[:, :n], in_=bk[:, :n]), reads=[br], writes=[tr])
            op("act", lambda e: e.activation(out=t2[:, :n], in_=bk[:, :n], func=AF.Copy), reads=[br], writes=[tr2])
            dma("sp", kT_d[j, c, :, xcol:xcol + n], t[:, :n], reads=[tr], writes=[res("kout")])
            dma("sp", ks_d[c, :, xcol:xcol + n], t2[:, :n], reads=[tr2], writes=[ksr])
        if DBG_ATTN < 1:
            return
        proj_fm(s, wqkv, D, KC, k_evac)
        if DBG_ATTN < 2:
            return
        toks = []
        for (xcol, lcol, n, kind) in subs:
            if kind == "p":
                toks += [(xcol + 128 * i, lcol + 128 * i, 128) for i in range(n // 128)]
            else:
                toks += [(xcol + DSEQ * q, lcol + DSEQ * q, DSEQ) for q in range(NSAMP)]
        for g in range(4):
            wv, wr = wq.request(wqkv[:, 2 * D + 256 * g:2 * D + 256 * g + 256], KC, 256)
            for (xc, lc, m) in toks:
                bk, br = dbank.next()
                for k in range(KC):
                    op("pe", lambda e, k=k, bk=bk: e.matmul(bk[:m, 0:256], lhsT=H[:, k, lc:lc + m], rhs=wv[:, k, :], start=(k == 0), stop=(k == KC - 1)),
                       reads=[wr, hr(lc - lc % 512 if lc < 1024 else 1024)], writes=[br], inc=(k == KC - 1))
                t, tr = tf.next()
                t2, tr2 = tb.next()
                op("dve", lambda e, t=t, bk=bk: e.tensor_copy(out=t[:m, 0:256], in_=bk[:m, 0:256]), reads=[br], writes=[tr])
                op("act", lambda e, t2=t2, bk=bk: e.activation(out=t2[:m, 0:256], in_=bk[:m, 0:256], func=AF.Copy), reads=[br], writes=[tr2])
                dma("sp", v_d[j, xc:xc + m, 256 * g:256 * g + 256], t[:m, 0:256], reads=[tr], writes=[res("vout")])
                dma("sp", vs_d[xc:xc + m, 256 * g:256 * g + 256], t2[:m, 0:256], reads=[tr2], writes=[vsr])
        attn_run(s, layer, ksr, vsr)
        proj_fm(s, W["attn_w_o"][j], 0, KC, resid_evac())

    def zero_other_half(c):
        o = (1 - c) * 64
        for (ktv, ktr) in ktl.items:
            op("dve", lambda e, ktv=ktv: e.memset(ktv[o:o + 64, :], 0.0), writes=[ktr])
        op("dve", lambda e: e.memset(kown[o:o + 64, :], 0.0), writes=[res("kown")])

    def attn_run(s, layer, ksr, vsr):
        j = layer // 2
        accbanks = [(banks[i], bank_r[i]) for i in range(4)]
        stb = Rot([(banks[4], bank_r[4]), (banks[5], bank_r[5]), (banks[6], bank_r[6])])
        kor, vor2 = res("kown"), res("vown")
        units = []
        for qi in range(2):
            gq = 2 * s + qi
            for hh in range(KC):
                for c in range(2):
                    units.append(dict(kind="p", hh=hh, c=c, gq=gq, qcol=512 * qi, nq=512, nqt=4, kv={},
                                      tiles=[(kb, kt) for kb in range(gq + 1) for kt in range(4)]))
        if s == 0:
            for q in range(NSAMP):
                for hh in range(KC):
                    for c in range(2):
                        units.append(dict(kind="s", hh=hh, c=c, q=q, gq=-1, qcol=1024 + q * DSEQ, nq=DSEQ, nqt=1, kv={},
                                          tiles=[(kb, kt) for kb in range(2) for kt in range(4)] + [(2, 0)]))
        LA = 2
        flat = [(u, ti) for u in units for ti in range(len(u["tiles"]))]
        kbl = [(u, kb) for u in units for kb in sorted(set(t[0] for t in u["tiles"]))]
        kpos = {(id(u), kb): m for m, (u, kb) in enumerate(kbl)}
        kzero = {}
        k3 = rstd_t[:, 0:256].bitcast(BF16)
        v3 = L["tf"][2][:, 0:260].bitcast(BF16).rearrange("p (t e) -> p t e", t=4)
        ktl3 = Rot(ktl.items + [(k3, res("rstd"))])
        vtl3 = Rot(vtl.items + [(v3, res("tf2"))])
        op("pool", lambda e: e.memset(v3[:, :, 128:130], 1.0), writes=[res("tf2")])

        def emit_load(m):
            u, kb = kbl[m]
            c, hh = u["c"], u["hh"]
            o = (1 - c) * 64
            if u["kind"] == "s" and kb == 2:
                xq = SEQ + u["q"] * DSEQ
                op("pool", lambda e: e.memset(kown[o:o + 64, :], 0.0), writes=[kor])
                dma("sp", kown[c * 64:(c + 1) * 64, :], ks_d[hh, c * 64:(c + 1) * 64, xq:xq + DSEQ], reads=[ksr], writes=[kor])
                dma("sp", vown[:, 0:128], vs_d[xq:xq + DSEQ, 128 * hh:128 * hh + 128], reads=[vsr], writes=[vor2])
                return
            ktv, ktr = ktl3.next()
            vtv, vtr = vtl3.next()
            if kzero.get(ktr.name) != c:
                op("pool", lambda e: e.memset(ktv[o:o + 64, :], 0.0), writes=[ktr])
                kzero[ktr.name] = c
            if u["kind"] == "p":
                dma("sp", ktv[c * 64:(c + 1) * 64, :], ks_d[hh, c * 64:(c + 1) * 64, 512 * kb:512 * kb + 512], reads=[ksr], writes=[ktr])
                dma("sp", vtv[:, :, 0:128], vs_d[512 * kb:512 * kb + 512, 128 * hh:128 * hh + 128].rearrange("(t p) e -> p t e", p=128),
                    reads=[vsr], writes=[vtr])
            else:
                q = u["q"]
                dma("pool", ktv[c * 64:(c + 1) * 64, :], ck_d[j, q, hh, c * 64:(c + 1) * 64, 512 * kb:512 * kb + 512], writes=[ktr])
                dma("pool", vtv[:, :, 0:128], cv_d[j, q, 512 * kb:512 * kb + 512, 128 * hh:128 * hh + 128].rearrange("(t p) e -> p t e", p=128), writes=[vtr])
            u["kv"][kb] = (ktv, ktr, vtv, vtr)

        def emit_st(u, ti):
            kb, kt = u["tiles"][ti]
            c, hh, nq, qcol = u["c"], u["hh"], u["nq"], u["qcol"]
            if ti == 0:
                u["started"] = [False] * u["nqt"]
            m = kpos[(id(u), kb)]
            if m == 0 and kt == 0:
                emit_load(0)
                if len(kbl) > 1:
                    emit_load(1)
            nkt = sum(1 for t_ in u["tiles"] if t_[0] == kb)
            if kt == min(LA, nkt - 1) and m + 2 < len(kbl):
                emit_load(m + 2)
            sbk, sbr = stb.next()
            if u["kind"] == "s" and kb == 2:
                op("pe", lambda e: e.matmul(sbk[:DSEQ, 0:nq], lhsT=kown[:, :], rhs=Aq[:, hh, qcol:qcol + nq], start=True, stop=True),
                   reads=[kor, aqr], writes=[sbr])
                return sbk, sbr
            ktv, ktr, vtv, vtr = u["kv"][kb]
            q0 = kt * 128 if kb == u["gq"] else 0
            op("pe", lambda e: e.matmul(sbk[:, q0:nq], lhsT=ktv[:, kt * 128:(kt + 1) * 128], rhs=Aq[:, hh, qcol + q0:qcol + nq], start=True, stop=True),
               reads=[ktr, aqr], writes=[sbr])
            return sbk, sbr

        def emit_exp_pv(u, ti, sb_):
            sbk, sbr = sb_
            kb, kt = u["tiles"][ti]
            nq, nqt = u["nq"], u["nqt"]
            diag = (kb == u["gq"])
            q0 = kt * 128 if diag else 0
            pt, ptr = tb.next()
            own = (u["kind"] == "s" and kb == 2)
            nk = DSEQ if own else 128
            if not diag:
                op("act", lambda e: e.activation(out=pt[:nk, 0:nq], in_=sbk[:nk, 0:nq], func=AF.Exp, scale=0.125), reads=[sbr], writes=[ptr])
            else:
                op("act", lambda e: e.activation(out=pt[0:64, q0:nq], in_=sbk[0:64, q0:nq], func=AF.Exp, scale=0.125), reads=[sbr], writes=[ptr])
                op("act", lambda e: e.activation(out=pt[64:128, q0 + 64:nq], in_=sbk[64:128, q0 + 64:nq], func=AF.Exp, scale=0.125), reads=[sbr], writes=[ptr])
                op("dve", lambda e: e.memset(pt[64:128, q0:q0 + 64], 0.0), writes=[ptr])
            last_tile = (ti == len(u["tiles"]) - 1)
            if own:
                ab, abr = accbanks[0]
                op("pe", lambda e: e.matmul(ab[:nq, 0:129], lhsT=pt[:DSEQ, 0:nq], rhs=vown[:, 0:129], start=False, stop=True),
                   reads=[ptr, vor2], writes=[abr])
            else:
                vtv, vtr = u["kv"][kb][2], u["kv"][kb][3]
                for qt in (range(kt, nqt) if diag else range(nqt)):
                    ab, abr = accbanks[qt]
                    w = min(128, nq - 128 * qt)
                    last = diag and kt == qt
                    op("pe", lambda e, ab=ab, qt=qt, w=w, st=(not u["started"][qt]), last=last:
                       e.matmul(ab[:w, 0:129], lhsT=pt[:, qt * 128:qt * 128 + w], rhs=vtv[:, kt, 0:129], start=st, stop=last),
                       reads=[ptr, vtr], writes=[abr])
                    u["started"][qt] = True
            if last_tile:
                lcols = [u["qcol"] + 128 * i for i in range(nqt)]
                return evac_acc(accbanks[:nqt], min(128, nq), u["c"], u["hh"], lcols, layer)
            return []

        stq = {}
        pending = []
        n = len(flat)

        def tick(force=False):
            nonlocal pending
            cur, pending = pending, []
            for (cd, f) in cur:
                if cd <= 1 or force:
                    for st in (f() or []):
                        pending.append(st)
                else:
                    pending.append((cd - 1, f))

        for i in range(n + LA):
            if i < n:
                stq[i] = emit_st(*flat[i])
            jx = i - LA
            if jx >= 0:
                tick()
                for st in emit_exp_pv(flat[jx][0], flat[jx][1], stq.pop(jx)):
                    pending.append(st)
        while pending:
            tick(force=True)

    phases = []
    for layer in range(DEPTH):
        for s in range(NSUP):
            phases.append(lambda s=s, layer=layer: ffn(s, layer, "ffn1"))
        for s in range(NSUP):
            if layer % 2 == 0:
                phases.append(lambda s=s, layer=layer: conv_layer(s, layer))
            else:
                phases.append(lambda s=s, layer=layer: attn_layer(s, layer))
        for s in range(NSUP):
            phases.append(lambda s=s, layer=layer: ffn(s, layer, "ffn2"))
    for s in range(NSUP):
        phases.append(lambda s=s: rmsnorm(s, "final_norm", final=True))
    if DBG_PHASES is not None:
        phases = phases[:DBG_PHASES] + phases[-NSUP:]
    for ph_fn in phases:
        ph_fn()
    outs = [res(n) for n in ("yout", "kout", "vout", "cvout")]
    deps = {}
    for r in outs:
        for k, v in r.w.items():
            deps[k] = max(deps.get(k, 0), v)
    ctx._wait("sp", deps)


def _fm(vec):
    v = np.asarray(vec, np.float32).reshape(-1, 128)
    return np.ascontiguousarray(v.T)


def _prep(inp):
    f32 = np.float32
    xp = np.asarray(inp["x_prompt"], f32)
    xs = np.asarray(inp["x_sample"], f32)
    ck = np.asarray(inp["cache_k"], f32)
    cv = np.asarray(inp["cache_v"], f32)
    sc = np.asarray(inp["state_conv"], f32)
    cst = np.zeros((128, NCONST), f32)

    def put(name, arr):
        cst[:, COFF[name]:COFF[name] + arr.shape[1]] = arr
    for i in range(DEPTH):
        put(f"ffn1_norm{i}", _fm(inp["ffn1_norm"][i]))
        put(f"mix_norm{i}", _fm(inp["mix_norm"][i]))
        put(f"ffn2_norm{i}", _fm(inp["ffn2_norm"][i]))
    put("final_norm", _fm(inp["final_norm"]))
    for j in range((DEPTH + 1) // 2):
        put(f"b_pw1{j}", _fm(inp["conv_b_pw1"][j]))
        put(f"b_dw{j}", _fm(inp["conv_b_dw"][j]))
        put(f"ln_g{j}", _fm(inp["conv_ln_g"][j]))
        put(f"ln_b{j}", _fm(inp["conv_ln_b"][j]))
        put(f"b_pw2{j}", _fm(inp["conv_b_pw2"][j]))
        wdw = np.asarray(inp["conv_w_dw"][j], f32)
        put(f"w_dw{j}", np.ascontiguousarray(wdw.T.reshape(8, 128, CW).transpose(1, 0, 2).reshape(128, 8 * CW)))
    for j in range(DEPTH // 2):
        for nm, key in (("lq1", "attn_lambda_q1"), ("lk1", "attn_lambda_k1"), ("lq2", "attn_lambda_q2"), ("lk2", "attn_lambda_k2")):
            put(f"{nm}{j}", np.broadcast_to(np.asarray(inp[key][j], f32)[None, :], (128, 64)))
        put(f"subln{j}", np.broadcast_to(np.asarray(inp["attn_subln"][j], f32)[None, :], (128, 128)))
    put("ident", np.eye(128, dtype=f32))
    wnames = ["ffn1_w_gate", "ffn1_w_up", "ffn1_w_down", "ffn2_w_gate", "ffn2_w_up", "ffn2_w_down",
              "conv_w_pw1", "conv_w_pw2", "attn_w_qkv", "attn_w_o"]
    wts = {n: np.ascontiguousarray(np.asarray(inp[n], f32)) for n in wnames}
    in_maps = []
    for c in range(8):
        p = c // 2
        xpc = xp[p]
        tok = np.concatenate([xpc, xs[2 * c], xs[2 * c + 1]], axis=0)
        xT = np.ascontiguousarray(tok.T.reshape(8, 128, NT).transpose(1, 0, 2))
        ckc = ck[:, 2 * c:2 * c + 2]
        ckc = np.ascontiguousarray(ckc.reshape(NA(), 2, PAST, 8, 128).transpose(0, 1, 3, 4, 2))
        cvc = np.ascontiguousarray(cv[:, 2 * c:2 * c + 2].reshape(NA(), 2, PAST, D))
        scc = sc[:, 2 * c:2 * c + 2]
        scc = np.ascontiguousarray(scc.reshape(NCV(), 2, CS, 8, 128).transpose(0, 1, 4, 3, 2))
        m = {"xT": xT, "ck": ckc, "cv": cvc, "sc": scc, "cst": cst}
        m.update(wts)
        in_maps.append(m)
    return in_maps


def kernel(**inp):
    in_maps = _prep(inp)
    if "nc" not in _NC_CACHE:
        _NC_CACHE["nc"] = build_program()
    nc = _NC_CACHE["nc"]
    res = run_bass_kernel_spmd(nc, in_maps, core_ids=list(range(8)))
    return _assemble(res.results)


def _assemble(R):
    f32 = np.float32
    B, DB = 4, 16
    y_prompt = np.zeros((B, SEQ, D), f32)
    y_sample = np.zeros((DB, DSEQ, D), f32)
    nkp = np.zeros((NA(), B, SEQ, 16, 64), f32)
    nvp = np.zeros((NA(), B, SEQ, 8, 128), f32)
    ncp = np.zeros((NCV(), B, CS, D), f32)
    nks = np.zeros((NA(), DB, DSEQ, 16, 64), f32)
    nvs = np.zeros((NA(), DB, DSEQ, 8, 128), f32)
    ncs = np.zeros((NCV(), DB, CS, D), f32)
    for c in range(8):
        r = R[c]
        y = np.asarray(r["yT"]).reshape(128, 8, NT).transpose(2, 1, 0).reshape(NT, D)
        kk = np.asarray(r["kT_out"]).reshape(NA(), 8, 128, NT).transpose(0, 3, 1, 2).reshape(NA(), NT, D)
        vv = np.asarray(r["v_out"]).reshape(NA(), NT, D)
        co = np.asarray(r["conv_out"]).reshape(NCV(), 3, 128, 8, CS).transpose(0, 1, 4, 3, 2).reshape(NCV(), 3, CS, D)
        if c % 2 == 0:
            p = c // 2
            y_prompt[p] = y[:SEQ]
            nkp[:, p] = kk[:, :SEQ].reshape(NA(), SEQ, 16, 64)
            nvp[:, p] = vv[:, :SEQ].reshape(NA(), SEQ, 8, 128)
            ncp[:, p] = co[:, 0]
        for q in range(2):
            b = 2 * c + q
            sl = slice(SEQ + q * DSEQ, SEQ + (q + 1) * DSEQ)
            y_sample[b] = y[sl]
            nks[:, b] = kk[:, sl].reshape(NA(), DSEQ, 16, 64)
            nvs[:, b] = vv[:, sl].reshape(NA(), DSEQ, 8, 128)
            ncs[:, b] = co[:, 1 + q]
    return (y_prompt, y_sample, nkp, nvp, ncp, nks, nvs, ncs)
```
